# Optimizing a Trainium2 kernel written in Bass

```python
import jax, jax.numpy as jnp
from jax import lax
import numpy as np

D_MODEL = 1024
BATCH = 4
SEQ = 4096
DEPTH = 2

CTX_LEN = 256
GRID_W = 64
ROPE_BASE = 10000.0
NORM_EPS = 1e-6

D_RNN = D_MODEL
LRU_BLOCKS = 8
LRU_BLOCK_W = D_RNN // LRU_BLOCKS
CONV_W = 4
CONV_PAD_L = 2
CONV_PAD_R = 1
LRU_C = 8.0

D_RET = D_MODEL
RET_HEADS = 4
RET_HEAD_DIM = D_RET // RET_HEADS
RET_CHUNK = 128

MLA_HEADS = 8
MLA_NOPE = 128
MLA_ROPE = 64
MLA_V = 128
D_MLA = MLA_HEADS * MLA_V
Q_LORA = 384
KV_LORA = 256
MLA_SCALE = (MLA_NOPE + MLA_ROPE) ** -0.5
Q_BLOCK = 128

N_BRANCH = 3
IN_SPLITS = (D_RNN, D_RNN, D_RET, D_RET, D_RET, D_RET, Q_LORA, KV_LORA, MLA_ROPE, D_MLA, N_BRANCH * D_MODEL)
IN_COLS = sum(IN_SPLITS)

kernel_name = 'hybrid_lru_retention_mla_prefix_dit'


def _rms(x, gain):
    xf = x.astype(jnp.float32)
    y = xf * lax.rsqrt(jnp.mean(xf * xf, axis=-1, keepdims=True) + NORM_EPS)
    return (y * gain.astype(jnp.float32)).astype(x.dtype)


def _split_cols(p):
    idx, acc = [], 0
    for s in IN_SPLITS[:-1]:
        acc += s
        idx.append(acc)
    return jnp.split(p, idx, axis=-1)


def _flip(t, rev, axis=1):
    return jnp.flip(t, axis=axis) if rev else t


def _axial_rope_tables(n_tokens, dim):
    rows = n_tokens // GRID_W
    row = jnp.repeat(jnp.arange(rows), GRID_W).astype(jnp.float32)
    col = jnp.tile(jnp.arange(GRID_W), rows).astype(jnp.float32)
    n_freq = dim // 4
    inv = ROPE_BASE ** (-jnp.arange(n_freq, dtype=jnp.float32) / n_freq)
    ang_r = row[:, None] * inv[None, :]
    ang_c = col[:, None] * inv[None, :]
    ang = jnp.concatenate([ang_r, ang_r, ang_c, ang_c], axis=-1)
    return jnp.cos(ang), jnp.sin(ang)


def _rope(x, cos, sin):
    q = x.shape[-1] // 4
    x1, x2, x3, x4 = x[..., :q], x[..., q:2 * q], x[..., 2 * q:3 * q], x[..., 3 * q:]
    rot = jnp.concatenate([-x2, x1, -x4, x3], axis=-1)
    return (x * cos + rot * sin).astype(x.dtype)


def _dwconv(u, w, b):
    T = u.shape[1]
    up = jnp.pad(u, ((0, 0), (CONV_PAD_L, CONV_PAD_R), (0, 0)))
    out = up[:, 0:T] * w[0]
    for j in range(1, CONV_W):
        out = out + up[:, j:j + T] * w[j]
    return out + b


def _lru_coeffs(u, wa, ba, wx, bx, lam):
    B, T, _ = u.shape
    ub = u.reshape(B, T, LRU_BLOCKS, LRU_BLOCK_W)
    r = jax.nn.sigmoid(jnp.einsum('btnc,ncd->btnd', ub, wa).reshape(B, T, D_RNN) + ba)
    i = jax.nn.sigmoid(jnp.einsum('btnc,ncd->btnd', ub, wx).reshape(B, T, D_RNN) + bx)
    log_a = -LRU_C * r * jax.nn.softplus(-lam.astype(jnp.float32))
    a = jnp.exp(log_a)
    b = jnp.sqrt(-jnp.expm1(2.0 * log_a)) * (i * u)
    return a, b


def _linear_scan(a, b, h0):
    def comb(l, r):
        return (l[0] * r[0], r[0] * l[1] + r[1])
    a_cum, h = lax.associative_scan(comb, (a, b), axis=1)
    return h + a_cum * h0[:, None, :]


def _rglru_branch(xa, xa_c, conv_w, conv_b, wa, ba, wx, bx, lam, with_ctx):
    u = _dwconv(xa.astype(jnp.float32), conv_w, conv_b)
    uc = _dwconv(xa_c.astype(jnp.float32), conv_w, conv_b)
    h_sum = jnp.zeros_like(u)
    hc_parts = []
    for d in range(2):
        rev = d == 1
        a, b = _lru_coeffs(u, wa[d], ba[d], wx[d], bx[d], lam[d])
        ac, bc = _lru_coeffs(uc, wa[d], ba[d], wx[d], bx[d], lam[d])
        hc = _linear_scan(_flip(ac, rev), _flip(bc, rev), jnp.zeros_like(uc[:, 0]))
        h = _linear_scan(_flip(a, rev), _flip(b, rev), hc[:, -1])
        h_sum = h_sum + _flip(h, rev)
        if with_ctx:
            hc_parts.append(_flip(hc, rev))
    y = h_sum.astype(xa.dtype)
    yc = (hc_parts[0] + hc_parts[1]).astype(xa.dtype) if with_ctx else None
    return y, yc


def _retention_chunks(q, k, v, log_g, s0):
    B, H, T, _ = q.shape
    dv = v.shape[-1]
    n = T // RET_CHUNK
    pos = jnp.arange(RET_CHUNK, dtype=jnp.float32)
    diff = pos[:, None] - pos[None, :]
    inner = jnp.where(diff >= 0, jnp.exp(log_g[:, None, None] * jnp.maximum(diff, 0.0)), 0.0)
    q_dec = jnp.exp(log_g[:, None] * (pos + 1.0))[None, :, :, None]
    k_dec = jnp.exp(log_g[:, None] * (RET_CHUNK - 1.0 - pos))[None, :, :, None]
    c_dec = jnp.exp(log_g * RET_CHUNK)[None, :, None, None]

    def chunks(t):
        return jnp.moveaxis(t.reshape(B, H, n, RET_CHUNK, t.shape[-1]), 2, 0)

    def step(s, blk):
        qc, kc, vc = blk
        att = jnp.einsum('bhnd,bhmd->bhnm', qc, kc) * inner
        o = jnp.einsum('bhnm,bhme->bhne', att, vc) + jnp.einsum('bhnd,bhde->bhne', qc, s) * q_dec
        s = s * c_dec + jnp.einsum('bhmd,bhme->bhde', kc * k_dec, vc)
        return s, o

    s, o = lax.scan(step, s0, (chunks(q), chunks(k), chunks(v)))
    return jnp.moveaxis(o, 0, 2).reshape(B, H, T, dv), s


def _head_layernorm(o, gain):
    B, H, T, dv = o.shape
    oc = o - jnp.mean(o, axis=-1, keepdims=True)
    y = oc * lax.rsqrt(jnp.mean(oc * oc, axis=-1, keepdims=True) + NORM_EPS)
    y = y * gain.astype(jnp.float32).reshape(H, 1, dv)
    return y.transpose(0, 2, 1, 3).reshape(B, T, H * dv)


def _retention_branch(q, k, v, qc, kc, vc, theta, gain, rope, with_ctx):
    cos, sin = rope
    B = q.shape[0]
    k_scale = RET_HEAD_DIM ** -0.5

    def heads(t):
        return t.astype(jnp.float32).reshape(t.shape[0], t.shape[1], RET_HEADS, RET_HEAD_DIM)

    def bhtd(t):
        return t.transpose(0, 2, 1, 3)

    q_l = bhtd(_rope(heads(q), cos[:, None], sin[:, None]))
    k_l = bhtd(_rope(heads(k), cos[:, None], sin[:, None]) * k_scale)
    v_l = bhtd(heads(v))
    q_c, k_c, v_c = bhtd(heads(qc)), bhtd(heads(kc) * k_scale), bhtd(heads(vc))
    s0 = jnp.zeros((B, RET_HEADS, RET_HEAD_DIM, RET_HEAD_DIM), jnp.float32)
    o_sum = jnp.zeros_like(v_l)
    oc_parts = []
    for d in range(2):
        rev = d == 1
        log_g = jax.nn.log_sigmoid(theta[d].astype(jnp.float32))
        oc, s_ctx = _retention_chunks(_flip(q_c, rev, 2), _flip(k_c, rev, 2), _flip(v_c, rev, 2), log_g, s0)
        o, _ = _retention_chunks(_flip(q_l, rev, 2), _flip(k_l, rev, 2), _flip(v_l, rev, 2), log_g, s_ctx)
        o_sum = o_sum + _flip(o, rev, 2)
        if with_ctx:
            oc_parts.append(_flip(oc, rev, 2))
    y = _head_layernorm(o_sum, gain).astype(q.dtype)
    yc = _head_layernorm(oc_parts[0] + oc_parts[1], gain).astype(q.dtype) if with_ctx else None
    return y, yc


def _mla_qkv(qd, kvd, kr, q_norm, w_q_up, kv_norm, w_kv_up, g_qn, g_qr, g_kn, g_kr, rope):
    B, T, _ = qd.shape
    q = (_rms(qd, q_norm) @ w_q_up).reshape(B, T, MLA_HEADS, MLA_NOPE + MLA_ROPE)
    kv = (_rms(kvd, kv_norm) @ w_kv_up).reshape(B, T, MLA_HEADS, MLA_NOPE + MLA_V)
    q_nope = _rms(q[..., :MLA_NOPE], g_qn)
    q_rope = _rms(q[..., MLA_NOPE:], g_qr)
    k_nope = _rms(kv[..., :MLA_NOPE], g_kn)
    v = kv[..., MLA_NOPE:]
    k_rope = _rms(kr, g_kr)
    if rope is not None:
        cos, sin = rope
        q_rope = _rope(q_rope, cos[:, None], sin[:, None])
        k_rope = _rope(k_rope, cos, sin)
    return q_nope, q_rope, k_nope, k_rope, v


def _mla_attend(q_nope, q_rope, k_nope, k_rope, v):
    s = (jnp.einsum('bqhd,bkhd->bhqk', q_nope, k_nope)
         + jnp.einsum('bqhr,bkr->bhqk', q_rope, k_rope)).astype(jnp.float32) * MLA_SCALE
    p = jax.nn.softmax(s, axis=-1)
    return jnp.einsum('bhqk,bkhe->bqhe', p.astype(v.dtype), v)


def _mla_latent(q_nope, q_rope, k_nope, k_rope, v):
    B, T = q_nope.shape[0], q_nope.shape[1]
    nb = T // Q_BLOCK

    def blocks(t):
        return jnp.moveaxis(t.reshape((B, nb, Q_BLOCK) + t.shape[2:]), 1, 0)

    out = lax.map(lambda qs: _mla_attend(qs[0], qs[1], k_nope, k_rope, v), (blocks(q_nope), blocks(q_rope)))
    return jnp.moveaxis(out, 0, 1).reshape(B, T, D_MLA)


def _merge(lp, ya, yb, yc, ga, gb, gc, gm):
    m_a, m_b, m_c = jnp.split(jax.nn.sigmoid(gm), 3, axis=-1)
    z = (m_a * ((ya * jax.nn.silu(ga)) @ lp['w_down_a'])
         + m_b * ((yb * jax.nn.silu(gb)) @ lp['w_down_b'])
         + m_c * ((yc * jax.nn.silu(gc)) @ lp['w_down_c']))
    return z @ lp['w_out']


def _layer(x, xc, c, c_ctx, lp, rope_ret, rope_mla, with_ctx):
    B, L = xc.shape[0], xc.shape[1]
    mod = jax.nn.silu(c) @ lp['w_mod'] + lp['b_mod']
    mod_c = jax.nn.silu(c_ctx) @ lp['w_mod'] + lp['b_mod']
    shift, scale, gate = jnp.split(mod, 3, axis=-1)
    shift_c, scale_c, gate_c = jnp.split(mod_c, 3, axis=-1)
    h = _rms(x, lp['norm_gain']) * (1.0 + scale[:, None]) + shift[:, None]
    hc = _rms(xc, lp['norm_gain']) * (1.0 + scale_c) + shift_c
    (xa, ga, qb, kb, vb, gb, qd, kvd, kr, gc, gm) = _split_cols(h @ lp['w_in'])
    (xa_c, ga_c, qb_c, kb_c, vb_c, gb_c, qd_c, kvd_c, kr_c, gc_c, gm_c) = _split_cols(hc @ lp['w_in'])

    ya, ya_c = _rglru_branch(xa, xa_c, lp['conv_w'], lp['conv_b'], lp['lru_wa'], lp['lru_ba'],
                             lp['lru_wx'], lp['lru_bx'], lp['lru_lambda'], with_ctx)
    yb, yb_c = _retention_branch(qb, kb, vb, qb_c, kb_c, vb_c, lp['ret_theta'], lp['ret_gain'], rope_ret, with_ctx)

    mla_w = (lp['mla_q_norm'], lp['mla_w_q_up'], lp['mla_kv_norm'], lp['mla_w_kv_up'],
             lp['mla_g_qn'], lp['mla_g_qr'], lp['mla_g_kn'], lp['mla_g_kr'])
    qn, qr, kn, krot, v = _mla_qkv(qd, kvd, kr, *mla_w, rope_mla)
    qn_c, qr_c, kn_c, krot_c, v_c = _mla_qkv(qd_c, kvd_c, kr_c, *mla_w, None)
    yc = _mla_latent(qn, qr, jnp.concatenate([kn_c, kn], axis=1),
                     jnp.concatenate([krot_c, krot], axis=1), jnp.concatenate([v_c, v], axis=1))

    x = x + (gate[:, None] * _merge(lp, ya, yb, yc, ga, gb, gc, gm)).astype(x.dtype)
    if with_ctx:
        yc_c = _mla_attend(qn_c, qr_c, kn_c, krot_c, v_c).reshape(B, L, D_MLA)
        xc = xc + (gate_c * _merge(lp, ya_c, yb_c, yc_c, ga_c, gb_c, gc_c, gm_c)).astype(xc.dtype)
    return x, xc


def setup_inputs(seed: int = 0) -> dict:
    key = jax.random.key(seed)
    ks = jax.random.split(key, 32)
    f32 = jnp.float32

    def nrm(k, shape, fan):
        return jax.random.normal(k, shape, f32) * (fan ** -0.5)

    def small(k, shape):
        return jax.random.normal(k, shape, f32) * 0.02

    def gain(k, shape):
        return 1.0 + 0.02 * jax.random.normal(k, shape, f32)

    u = jax.random.uniform(ks[10], (DEPTH, 2, D_RNN), f32, minval=0.9, maxval=0.999)
    a0 = u ** (1.0 / LRU_C)
    lru_lambda = jnp.log(a0) - jnp.log1p(-a0)
    ret_base = jnp.log(2.0 ** (5.0 + jnp.arange(RET_HEADS, dtype=f32)) - 1.0)
    ret_theta = ret_base + 0.05 * jax.random.normal(ks[11], (DEPTH, 2, RET_HEADS), f32)

    return {
        'x': jax.random.normal(ks[0], (BATCH, SEQ, D_MODEL), f32),
        'c': jax.random.normal(ks[1], (BATCH, D_MODEL), f32),
        'ctx': jax.random.normal(ks[2], (BATCH, CTX_LEN, D_MODEL), f32),
        'c_ctx': jax.random.normal(ks[3], (D_MODEL,), f32),
        'w_mod': nrm(ks[4], (DEPTH, D_MODEL, 3 * D_MODEL), D_MODEL),
        'b_mod': small(ks[5], (DEPTH, 3 * D_MODEL)),
        'norm_gain': gain(ks[6], (DEPTH, D_MODEL)),
        'w_in': nrm(ks[7], (DEPTH, D_MODEL, IN_COLS), D_MODEL),
        'conv_w': nrm(ks[8], (DEPTH, CONV_W, D_RNN), CONV_W),
        'conv_b': small(ks[9], (DEPTH, D_RNN)),
        'lru_wa': nrm(ks[12], (DEPTH, 2, LRU_BLOCKS, LRU_BLOCK_W, LRU_BLOCK_W), LRU_BLOCK_W),
        'lru_ba': small(ks[13], (DEPTH, 2, D_RNN)),
        'lru_wx': nrm(ks[14], (DEPTH, 2, LRU_BLOCKS, LRU_BLOCK_W, LRU_BLOCK_W), LRU_BLOCK_W),
        'lru_bx': small(ks[15], (DEPTH, 2, D_RNN)),
        'lru_lambda': lru_lambda,
        'ret_theta': ret_theta,
        'ret_gain': gain(ks[16], (DEPTH, D_RET)),
        'mla_q_norm': gain(ks[17], (DEPTH, Q_LORA)),
        'mla_w_q_up': nrm(ks[18], (DEPTH, Q_LORA, MLA_HEADS * (MLA_NOPE + MLA_ROPE)), Q_LORA),
        'mla_kv_norm': gain(ks[19], (DEPTH, KV_LORA)),
        'mla_w_kv_up': nrm(ks[20], (DEPTH, KV_LORA, MLA_HEADS * (MLA_NOPE + MLA_V)), KV_LORA),
        'mla_g_qn': gain(ks[21], (DEPTH, MLA_NOPE)),
        'mla_g_qr': gain(ks[22], (DEPTH, MLA_ROPE)),
        'mla_g_kn': gain(ks[23], (DEPTH, MLA_NOPE)),
        'mla_g_kr': gain(ks[24], (DEPTH, MLA_ROPE)),
        'w_down_a': nrm(ks[25], (DEPTH, D_RNN, D_MODEL), D_RNN),
        'w_down_b': nrm(ks[26], (DEPTH, D_RET, D_MODEL), D_RET),
        'w_down_c': nrm(ks[27], (DEPTH, D_MLA, D_MODEL), D_MLA),
        'w_out': nrm(ks[28], (DEPTH, D_MODEL, D_MODEL), D_MODEL),
    }


def reference(x, c, ctx, c_ctx, w_mod, b_mod, norm_gain, w_in, conv_w, conv_b, lru_wa, lru_ba, lru_wx, lru_bx,
              lru_lambda, ret_theta, ret_gain, mla_q_norm, mla_w_q_up, mla_kv_norm, mla_w_kv_up,
              mla_g_qn, mla_g_qr, mla_g_kn, mla_g_kr, w_down_a, w_down_b, w_down_c, w_out):
    n_lat = x.shape[1]
    rope_ret = _axial_rope_tables(n_lat, RET_HEAD_DIM)
    rope_mla = _axial_rope_tables(n_lat, MLA_ROPE)
    xc = ctx
    for l in range(DEPTH):
        lp = {
            'w_mod': w_mod[l], 'b_mod': b_mod[l], 'norm_gain': norm_gain[l], 'w_in': w_in[l],
            'conv_w': conv_w[l], 'conv_b': conv_b[l], 'lru_wa': lru_wa[l], 'lru_ba': lru_ba[l],
            'lru_wx': lru_wx[l], 'lru_bx': lru_bx[l], 'lru_lambda': lru_lambda[l],
            'ret_theta': ret_theta[l], 'ret_gain': ret_gain[l],
            'mla_q_norm': mla_q_norm[l], 'mla_w_q_up': mla_w_q_up[l], 'mla_kv_norm': mla_kv_norm[l],
            'mla_w_kv_up': mla_w_kv_up[l], 'mla_g_qn': mla_g_qn[l], 'mla_g_qr': mla_g_qr[l],
            'mla_g_kn': mla_g_kn[l], 'mla_g_kr': mla_g_kr[l],
            'w_down_a': w_down_a[l], 'w_down_b': w_down_b[l], 'w_down_c': w_down_c[l], 'w_out': w_out[l],
        }
        x, xc = _layer(x, xc, c, c_ctx, lp, rope_ret, rope_mla, l < DEPTH - 1)
    return x
```

```python
import numpy as np
from contextlib import ExitStack
import ml_dtypes
import concourse.bass as bass
import concourse.mybir as mybir
from concourse.bass_utils import run_bass_kernel_spmd

F32 = mybir.dt.float32
BF16 = mybir.dt.bfloat16
AF = mybir.ActivationFunctionType
ALU = mybir.AluOpType

S = 4352
L = 256
TT = [(0, 256)] + [(256 + 512 * i, 512) for i in range(8)]
NCH = 34
EPS = 1e-6
XA, GA, QQ, KK, VV, GB, QD, KVD, KR, GC = 0, 512, 1024, 1536, 2048, 2560, 3072, 3456, 3712, 3776
NWIN = 4288
NTB = 2176
TB = [(0, 128)] + [(128 + 256 * i, 256) for i in range(8)]


class Sem:
    def __init__(self, h, name):
        self.h = h
        self.name = name
        self.val = 0


class Tile:
    def __init__(self, t, name):
        self.t = t
        self.name = name
        self.w = None
        self.r = {}
        self.dsem = None

    def __getitem__(self, k):
        return self.t[k]


class Eng:
    def __init__(self, name, eng, sem):
        self.name = name
        self.e = eng
        self.sem = sem
        self.seen = {}


class K:
    def __init__(self, nc, stack):
        self.nc = nc
        self.stack = stack
        self.engs = {}
        self.nsem = 0
        self.n_ops = 0
        self.sems = []
        for n in ['tensor', 'vector', 'scalar', 'gpsimd', 'sync']:
            self.engs[n] = Eng(n, getattr(nc, n), self.new_sem(n))

    def new_sem(self, name):
        self.nsem += 1
        assert self.nsem < 140, "too many semaphores"
        sm = Sem(self.stack.enter_context(self.nc.semaphore(f"s{self.nsem}_{name}")), name)
        self.sems.append(sm)
        return sm

    def barrier(self):
        for eng in self.engs.values():
            for s in self.sems:
                if s.val > 0 and eng.seen.get(s, 0) < s.val and not (s is eng.sem):
                    eng.e.wait_ge(s.h, s.val)
                    eng.seen[s] = s.val

    def sbuf(self, shape, dtype, name, stack=None):
        st = stack if stack is not None else self.stack
        self.n_t = getattr(self, "n_t", 0) + 1
        t = Tile(st.enter_context(self.nc.sbuf_tensor(f"{name}_{self.n_t}", list(shape), dtype)), name)
        t.base = name
        return t

    def psum(self, shape, dtype, name):
        t = Tile(self.stack.enter_context(self.nc.psum_tensor(name, list(shape), dtype)), name)
        t.excl = True
        return t

    def fake(self, name):
        t = Tile(None, name)
        t.base = name
        return t

    def collective(self, kind, groups, in_ap, out_ap, R=(), W=()):
        eng = self.engs['gpsimd']
        self._waits(eng, R, W)
        if not hasattr(self, "ccsem"):
            self.ccsem = self.new_sem('cc')
        s_ = self.ccsem
        inst = self.nc.gpsimd.collective_compute(kind, ALU.bypass, groups, ins=[in_ap], outs=[out_ap])
        s_.val += 1
        inst.then_inc(s_.h)
        for t in W:
            t.w = (s_, s_.val)
            t.r = {}
        for t in R:
            t.r[s_] = s_.val

    def _waits(self, eng, R, W):
        need = {}

        def add(s, v):
            if need.get(s, 0) < v:
                need[s] = v
        for t in R:
            if t.w is not None:
                add(*t.w)
        for t in W:
            if t.w is not None:
                add(*t.w)
            for s, v in t.r.items():
                add(s, v)
        for s, v in need.items():
            if s is eng.sem and eng.name == 'tensor':
                continue
            if eng.seen.get(s, 0) >= v:
                continue
            eng.e.wait_ge(s.h, v)
            eng.seen[s] = v

    def op(self, engname, fn, R=(), W=(), inc=True):
        eng = self.engs[engname]
        W = list(W) + [t for t in R if getattr(t, "excl", False) and t not in W]
        self._waits(eng, R, W)
        inst = fn(eng.e)
        if inc:
            eng.sem.val += 1
            inst.then_inc(eng.sem.h, 1)
            v = eng.sem.val
        else:
            assert engname == 'tensor'
            v = eng.sem.val + 1
        for t in W:
            t.w = (eng.sem, v)
            t.r = {}
        for t in R:
            if t in W:
                continue
            t.r[eng.sem] = v
        self.n_ops += 1
        return inst

    def dma(self, out, in_, R=(), W=(), q='sync'):
        eng = self.engs[q]
        self._waits(eng, R, W)
        owner = W[0] if len(W) else R[0]
        if owner.dsem is None:
            if not hasattr(self, "dsems"):
                self.dsems = {}
            base = getattr(owner, "base", owner.name)
            if base not in self.dsems:
                self.dsems[base] = self.new_sem('d_' + base)
            owner.dsem = self.dsems[base]
        s = owner.dsem
        inst = eng.e.dma_start(out=out, in_=in_)
        s.val += 16
        inst.then_inc(s.h, 16)
        for t in W:
            t.w = (s, s.val)
            t.r = {}
        for t in R:
            t.r[s] = s.val
        return inst

    def finish(self, tiles):
        eng = self.engs['sync']
        for t in tiles:
            self._waits(eng, [t], [t])


def rev_ap(ap):
    dims = [list(d) for d in ap.ap]
    step, n = dims[-1]
    dims[-1] = [-step, n]
    return bass.AP(ap.tensor, ap.offset + (n - 1) * step, dims)


class Ring:
    def __init__(self, tiles):
        self.tiles = tiles
        self.i = 0

    def get(self):
        t = self.tiles[self.i % len(self.tiles)]
        self.i += 1
        return t


def softplus_neg(k, stack, src, n, name):
    x = k.sbuf([128, n], F32, name + "_x", stack)
    y = k.sbuf([128, n], F32, name + "_y", stack)
    y2 = k.sbuf([128, n], F32, name + "_y2", stack)
    pl = k.sbuf([128, n], F32, name + "_p", stack)
    k.op('scalar', lambda e: e.activation(out=x[:], in_=src[:], func=AF.Exp, scale=-1.0), R=[src], W=[x])
    k.op('vector', lambda e: e.tensor_scalar(out=y[:], in0=x[:], scalar1=2.0, scalar2=None, op0=ALU.add), R=[x], W=[y])
    k.op('vector', lambda e: e.reciprocal(out=y[:], in_=y[:]), R=[y], W=[y])
    k.op('vector', lambda e: e.tensor_tensor(out=y[:], in0=y[:], in1=x[:], op=ALU.mult), R=[y, x], W=[y])
    k.op('vector', lambda e: e.tensor_tensor(out=y2[:], in0=y[:], in1=y[:], op=ALU.mult), R=[y], W=[y2])
    k.op('vector', lambda e: e.tensor_scalar(out=pl[:], in0=y2[:], scalar1=1.0 / 11.0, scalar2=1.0 / 9.0, op0=ALU.mult, op1=ALU.add), R=[y2], W=[pl])
    for cst in (1.0 / 7.0, 1.0 / 5.0, 1.0 / 3.0, 1.0):
        k.op('vector', lambda e: e.tensor_tensor(out=pl[:], in0=pl[:], in1=y2[:], op=ALU.mult), R=[pl, y2], W=[pl])
        k.op('vector', lambda e, cst=cst: e.tensor_scalar(out=pl[:], in0=pl[:], scalar1=cst, scalar2=None, op0=ALU.add), R=[pl], W=[pl])
    k.op('vector', lambda e: e.tensor_tensor(out=pl[:], in0=pl[:], in1=y[:], op=ALU.mult), R=[pl, y], W=[pl])
    k.op('vector', lambda e: e.tensor_scalar(out=pl[:], in0=pl[:], scalar1=2.0, scalar2=None, op0=ALU.mult), R=[pl], W=[pl])
    return pl


def emit_A(nc, k, l, din, dint, SH, phases=("lru", "ret", "mla"), nogate=False):
    xT = din("xT", [1024, S]).rearrange("(c p) t -> p c t", p=128)
    cT = din(f"cT_{l}", [128, 8, 2])
    wmod = din(f"wmod_{l}", [1024, 3072]).rearrange("(c p) n -> p c n", p=128)
    bmod = din(f"bmod_{l}", [128, 24])
    ngain = din(f"ngain_{l}", [128, 8])
    win = din(f"win_{l}", [1024, NWIN]).rearrange("(c p) n -> p c n", p=128)
    cw_d = din(f"convw_{l}", [128, 4, 4])
    cb_d = din(f"convb_{l}", [128, 4])
    wa_d = din(f"lru_wa_{l}", [128, 2, 4, 128])
    wx_d = din(f"lru_wx_{l}", [128, 2, 4, 128])
    ba_d = din(f"lru_ba_{l}", [128, 2, 4])
    bx_d = din(f"lru_bx_{l}", [128, 2, 4])
    lam_d = din(f"lru_lam_{l}", [128, 2, 4])
    th_d = din(f"ret_theta_{l}", [128, 2, 2])
    rg_d = din(f"ret_gain_{l}", [128, 2, 2])
    qn_d = din(f"mla_qnorm_{l}", [128, 3])
    wqup_d = din(f"mla_wqup_{l}", [384, 768]).rearrange("(c p) n -> p c n", p=128)
    kvn_d = din(f"mla_kvnorm_{l}", [128, 2])
    wkvup_d = din(f"mla_wkvup_{l}", [256, 1024]).rearrange("(c p) n -> p c n", p=128)
    gq_d = din(f"mla_g4_{l}", [128, 4])
    ident_d = din("c_ident", [128, 128])
    permR_d = din("c_permR", [128, 128])
    permM_d = din("c_permM", [128, 128])
    ropeR_d = din("c_ropeR", [128, 2, 64])
    ropeM_d = din("c_ropeM", [64, 2, 64])
    ropeRT_d = din("c_ropeRT", [128, 2, 2, 4096])
    ropeMT_d = din("c_ropeMT", [128, 2, 4096])
    cmask_d = din("c_mask", [128, 4, 128])
    cpos_d = din("c_pos", [128, 2, 128])
    kpos_d = din("c_kpos", [128, 2])

    GTl = dint("GTl", [2 * 1536, NTB], BF16)
    GT2 = [GTl[h_ * 1536:(h_ + 1) * 1536, :] for h_ in range(2)]
    hT2 = dint("hT2", [2 * 1024, NTB], BF16)
    hT2v = [hT2[h_ * 1024:(h_ + 1) * 1024, :].rearrange("(c p) t -> p c t", p=128) for h_ in range(2)]
    x1all = dint("x1all", [2 * 1024, NTB], F32)
    x1v = [x1all.rearrange("(c h p) t -> h p c t", h=2, p=128)[h_] for h_ in range(2)]

    def pieces(s0_, N):
        out_ = []
        for c0 in range(s0_, s0_ + N, 128):
            if c0 < L:
                out_.append((c0 - s0_, 128, c0 // 128, 0))
            else:
                t_ = c0 - L
                out_.append((c0 - s0_, 128, t_ // 2048, 128 + t_ % 2048))
        mg = [list(out_[0])]
        for (a_, n_, h_, o_) in out_[1:]:
            if h_ == mg[-1][2] and o_ == mg[-1][3] + mg[-1][1] and a_ == mg[-1][0] + mg[-1][1]:
                mg[-1][1] += n_
            else:
                mg.append([a_, n_, h_, o_])
        return [tuple(m_) for m_ in mg]

    if True:
        banks = SH["banks"]
        rot = Ring(banks[:6])
        ps = rot.get
        accO, accL = banks[6], banks[7]
        ones, ident, permR, permM = SH["ones"], SH["ident"], SH["permR"], SH["permM"]

        mod = k.sbuf([128, 24, 2], F32, "mod")
        gs = k.sbuf([128, 8, 2], F32, "gs")
        with ExitStack() as s0:
            wm = k.sbuf([128, 8, 3072], BF16, "wm", s0)
            for kc in range(8):
                k.dma(wm[:, kc, :], wmod[:, kc, :], W=[wm], q='gpsimd')
            ct = k.sbuf([128, 8, 2], F32, "ct", s0)
            cb = k.sbuf([128, 8, 2], BF16, "cb", s0)
            bm = k.sbuf([128, 24], F32, "bm", s0)
            ng = k.sbuf([128, 8], F32, "ng", s0)
            k.dma(ct[:], cT, W=[ct])
            k.dma(bm[:], bmod, W=[bm])
            k.dma(ng[:], ngain, W=[ng])
            k.op('scalar', lambda e: e.activation(out=cb[:], in_=ct[:], func=AF.Silu), R=[ct], W=[cb])
            pm = ps()
            for j in range(24):
                for kc in range(8):
                    k.op('tensor', lambda e, j=j, kc=kc: e.matmul(pm[:, 2 * j:2 * j + 2], lhsT=wm[:, kc, j * 128:(j + 1) * 128], rhs=cb[:, kc, :],
                                                                 start=(kc == 0), stop=(kc == 7)), R=[wm, cb], W=[pm], inc=(kc == 7))
            k.op('vector', lambda e: e.tensor_tensor(out=mod[:], in0=pm[:, 0:48].rearrange("p (j t) -> p j t", t=2),
                                                     in1=bm[:].unsqueeze(2).to_broadcast([128, 24, 2]), op=ALU.add), R=[pm, bm], W=[mod])
            k.op('vector', lambda e: e.tensor_scalar(out=gs[:], in0=mod[:, 8:16, :], scalar1=1.0, scalar2=None, op0=ALU.add), R=[mod], W=[gs])
            k.op('vector', lambda e: e.tensor_tensor(out=gs[:], in0=gs[:], in1=ng[:].unsqueeze(2).to_broadcast([128, 8, 2]), op=ALU.mult), R=[gs, ng], W=[gs])
            k.barrier()

        hD = [k.fake(f"hD{i}") for i in range(len(TT))]
        SH["HDall"] = hD
        with ExitStack() as s1:
            xr = Ring([k.sbuf([128, 8, 512], F32, f"xr{i}", s1) for i in range(2)])
            sqr = Ring([k.sbuf([128, 8, 512], BF16, f"sq{i}", s1) for i in range(2)])
            hor = Ring([k.sbuf([128, 8, 512], BF16, f"ho{i}", s1) for i in range(2)])
            rsr = Ring([k.sbuf([128, 512], F32, f"rs{i}", s1) for i in range(2)])
            for ti, (s0_, N) in enumerate(TT):
                col = 1 if ti == 0 else 0
                xt = xr.get(); sq = sqr.get(); ho = hor.get(); rs = rsr.get()
                if l == 0:
                    k.dma(xt[:, :, :N], xT[:, :, s0_:s0_ + N], W=[xt])
                else:
                    for (a_, n_, h_, o_) in pieces(s0_, N):
                        k.dma(xt[:, :, a_:a_ + n_], x1v[h_][:, :, o_:o_ + n_], R=[SH["X1ALL"]], W=[xt])
                k.op('scalar', lambda e: e.activation(out=sq[:, :, :N], in_=xt[:, :, :N], func=AF.Square), R=[xt], W=[sq])
                p = ps()
                for c in range(8):
                    k.op('tensor', lambda e, c=c: e.matmul(p[:, :N], lhsT=ones[:], rhs=sq[:, c, :N], start=(c == 0), stop=(c == 7)), R=[ones, sq], W=[p], inc=(c == 7))
                k.op('scalar', lambda e: e.activation(out=rs[:, :N], in_=p[:, :N], func=AF.Ln, scale=1.0 / 1024.0, bias=EPS), R=[p], W=[rs])
                k.op('scalar', lambda e: e.activation(out=rs[:, :N], in_=rs[:, :N], func=AF.Exp, scale=-0.5), R=[rs], W=[rs])
                k.op('vector', lambda e: e.tensor_tensor(out=xt[:, :, :N], in0=xt[:, :, :N], in1=rs[:, :N].unsqueeze(1).to_broadcast([128, 8, N]), op=ALU.mult), R=[xt, rs], W=[xt])
                for c in range(8):
                    k.op('vector' if c % 2 else 'scalar',
                         (lambda e, c=c: e.tensor_scalar(out=ho[:, c, :N], in0=xt[:, c, :N], scalar1=gs[:, c, col:col + 1], scalar2=mod[:, c, col:col + 1], op0=ALU.mult, op1=ALU.add)) if c % 2 else
                         (lambda e, c=c: e.activation(out=ho[:, c, :N], in_=xt[:, c, :N], func=AF.Identity, scale=gs[:, c, col:col + 1], bias=mod[:, c, col:col + 1])),
                         R=[xt, gs, mod], W=[ho])
                for (a_, n_, h_, o_) in pieces(s0_, N):
                    k.dma(hT2v[h_][:, :, o_:o_ + n_], ho[:, :, a_:a_ + n_], R=[ho], W=[hD[ti]])
            k.barrier()

        hring = SH["hring"]

        def load_h(ti):
            t = hring.get()
            s0_, N = TT[ti]
            for (a_, n_, h_, o_) in pieces(s0_, N):
                k.dma(t[:, :, a_:a_ + n_], hT2v[h_][:, :, o_:o_ + n_], R=[hD[ti]], W=[t])
            return t

        def proj(h, N, wt, c0, M):
            p = ps()
            for kc in range(8):
                k.op('tensor', lambda e, kc=kc: e.matmul(p[:M, :N], lhsT=wt[:, kc, c0:c0 + M], rhs=h[:, kc, :N], start=(kc == 0), stop=(kc == 7)), R=[wt, h], W=[p], inc=(kc == 7))
            return p

        GD = SH["GD"]
        GTall_ = dint("GTall", [2 * 3072, NTB], BF16)

        def gather_rows(chunks):
            for c_ in chunks:
                k.collective("AllGather", PAIRS, GTl[c_ * 256:(c_ + 1) * 256, :], GTall_[c_ * 512:(c_ + 1) * 512, :], R=[GD], W=[SH["GALL"]])

        if "lru" in phases:
            with ExitStack() as s2:
                wl = k.sbuf([128, 8, 1024], BF16, "wl", s2)
                for kc in range(8):
                    k.dma(wl[:, kc, :], win[:, kc, XA:XA + 1024], W=[wl], q='gpsimd')
                wa = k.sbuf([128, 2, 4, 128], BF16, "wa", s2)
                wx = k.sbuf([128, 2, 4, 128], BF16, "wx", s2)
                k.dma(wa[:], wa_d, W=[wa], q='gpsimd')
                k.dma(wx[:], wx_d, W=[wx], q='gpsimd')
                cw = k.sbuf([128, 4, 4], F32, "cw", s2)
                cbv = k.sbuf([128, 4], F32, "cbv", s2)
                hba = k.sbuf([128, 8], F32, "hba", s2)
                hbx = k.sbuf([128, 8], F32, "hbx", s2)
                lam = k.sbuf([128, 8], F32, "lam", s2)
                k.dma(cw[:], cw_d, W=[cw]); k.dma(cbv[:], cb_d, W=[cbv])
                k.dma(hba[:], ba_d.rearrange("p a b -> p (a b)"), W=[hba])
                k.dma(hbx[:], bx_d.rearrange("p a b -> p (a b)"), W=[hbx])
                k.dma(lam[:], lam_d.rearrange("p a b -> p (a b)"), W=[lam])
                k.op('vector', lambda e: e.tensor_scalar(out=hba[:], in0=hba[:], scalar1=0.5, scalar2=None, op0=ALU.mult), R=[hba], W=[hba])
                k.op('vector', lambda e: e.tensor_scalar(out=hbx[:], in0=hbx[:], scalar1=0.5, scalar2=None, op0=ALU.mult), R=[hbx], W=[hbx])
                hcl = softplus_neg(k, s2, lam, 8, "spl")
                k.op('vector', lambda e: e.tensor_scalar(out=hcl[:], in0=hcl[:], scalar1=-4.0, scalar2=None, op0=ALU.mult), R=[hcl], W=[hcl])
                XAb = k.sbuf([128, S], F32, "XAb", s2)
                U = k.sbuf([128, S], F32, "U", s2)
                UB = k.sbuf([128, S], BF16, "UB", s2)
                SG = k.sbuf([128, S], BF16, "SG", s2)
                I = k.sbuf([128, S], F32, "I", s2)
                Sb = k.sbuf([128, S], F32, "Sb", s2)
                HF = k.sbuf([128, S], F32, "HF", s2)
                OB = k.sbuf([128, S], BF16, "OBl", s2)
                R2 = k.sbuf([128, S], F32, "R2", s2)
                I2 = k.sbuf([128, S], F32, "I2", s2)
                Sb2 = k.sbuf([128, S], F32, "Sb2", s2)
                for j in range(4):
                    for ti, (s0_, N) in enumerate(TT):
                        h = load_h(ti)
                        p = proj(h, N, wl, j * 128, 128)
                        k.op('scalar', lambda e: e.activation(out=XAb[:, s0_:s0_ + N], in_=p[:, :N], func=AF.Copy), R=[p], W=[XAb])
                        p2 = proj(h, N, wl, 512 + j * 128, 128)
                        k.op('scalar', lambda e: e.activation(out=SG[:, s0_:s0_ + N], in_=p2[:, :N], func=(AF.Copy if nogate == 2 else AF.Silu)), R=[p2], W=[SG])
                    k.op('vector', lambda e: e.tensor_scalar(out=U[:], in0=XAb[:], scalar1=cw[:, j, 2:3], scalar2=cbv[:, j:j + 1], op0=ALU.mult, op1=ALU.add), R=[XAb, cw, cbv], W=[U])
                    for (a0, Ls) in ((0, L), (L, S - L)):
                        for tap, (do, so, ln) in ((0, (2, 0, Ls - 2)), (1, (1, 0, Ls - 1)), (3, (0, 1, Ls - 1))):
                            k.op('vector', lambda e, tap=tap, do=do, so=so, ln=ln, a0=a0: e.scalar_tensor_tensor(
                                out=U[:, a0 + do:a0 + do + ln], in0=XAb[:, a0 + so:a0 + so + ln], scalar=cw[:, j, tap:tap + 1], op0=ALU.mult,
                                in1=U[:, a0 + do:a0 + do + ln], op1=ALU.add), R=[XAb, U, cw], W=[U])
                    k.op('scalar', lambda e: e.activation(out=UB[:], in_=U[:], func=AF.Copy), R=[U], W=[UB])
                    Rbs, Is, Sbs = [XAb, R2], [I, I2], [Sb, Sb2]
                    for ti, (s0_, N) in enumerate(TT):
                        for d in range(2):
                            dj = d * 4 + j
                            p = ps()
                            k.op('tensor', lambda e, d=d: e.matmul(p[:, :N], lhsT=wa[:, d, j, :], rhs=UB[:, s0_:s0_ + N], start=True, stop=True), R=[wa, UB], W=[p])
                            k.op('scalar', lambda e, d=d, dj=dj: e.activation(out=Rbs[d][:, s0_:s0_ + N], in_=p[:, :N], func=AF.Tanh, scale=0.5, bias=hba[:, dj:dj + 1]), R=[p, hba], W=[Rbs[d]])
                            p2 = ps()
                            k.op('tensor', lambda e, d=d: e.matmul(p2[:, :N], lhsT=wx[:, d, j, :], rhs=UB[:, s0_:s0_ + N], start=True, stop=True), R=[wx, UB], W=[p2])
                            k.op('scalar', lambda e, d=d, dj=dj: e.activation(out=Is[d][:, s0_:s0_ + N], in_=p2[:, :N], func=AF.Tanh, scale=0.5, bias=hbx[:, dj:dj + 1]), R=[p2, hbx], W=[Is[d]])
                    for d in range(2):
                        dj = d * 4 + j
                        k.op('scalar', lambda e, d=d, dj=dj: e.activation(out=Rbs[d][:], in_=Rbs[d][:], func=AF.Exp, scale=hcl[:, dj:dj + 1], bias=hcl[:, dj:dj + 1]), R=[Rbs[d], hcl], W=[Rbs[d]])
                    for d in range(2):
                        k.op('scalar', lambda e, d=d: e.activation(out=Sbs[d][:], in_=Rbs[d][:], func=AF.Square), R=[Rbs[d]], W=[Sbs[d]])
                    for d in range(2):
                        k.op('vector', lambda e, d=d: e.scalar_tensor_tensor(out=Is[d][:], in0=Is[d][:], scalar=1.0, op0=ALU.add, in1=U[:], op1=ALU.mult), R=[Is[d], U], W=[Is[d]])
                    for d in range(2):
                        k.op('scalar', lambda e, d=d: e.activation(out=Sbs[d][:], in_=Sbs[d][:], func=AF.Sqrt, scale=-0.25, bias=0.25), R=[Sbs[d]], W=[Sbs[d]])
                    for d in range(2):
                        k.op('vector', lambda e, d=d: e.tensor_tensor(out=Is[d][:], in0=Is[d][:], in1=Sbs[d][:], op=ALU.mult), R=[Is[d], Sbs[d]], W=[Is[d]])
                    k.op('vector', lambda e: e.tensor_tensor_scan(out=HF[:], data0=Rbs[0][:], data1=Is[0][:], initial=0.0, op0=ALU.mult, op1=ALU.add), R=[Rbs[0], Is[0]], W=[HF])
                    k.op('vector', lambda e: e.tensor_tensor_scan(out=rev_ap(Sb2[:, 0:L]), data0=rev_ap(R2[:, 0:L]), data1=rev_ap(I2[:, 0:L]), initial=0.0,
                                                                  op0=ALU.mult, op1=ALU.add), R=[R2, I2], W=[Sb2])
                    k.op('vector', lambda e: e.tensor_tensor_scan(out=rev_ap(Sb2[:, L:S]), data0=rev_ap(R2[:, L:S]), data1=rev_ap(I2[:, L:S]), initial=Sb2[:, 0:1],
                                                                  op0=ALU.mult, op1=ALU.add), R=[R2, I2, Sb2], W=[Sb2])
                    k.op('gpsimd', lambda e: e.tensor_tensor(out=HF[:], in0=HF[:], in1=Sb2[:], op=ALU.add), R=[HF, Sb2], W=[HF])
                    if nogate:
                        k.op('vector', lambda e: e.tensor_copy(out=OB[:], in_=(SG[:] if nogate == 2 else HF[:])), R=[HF, SG], W=[OB])
                    else:
                        k.op('vector', lambda e: e.tensor_tensor(out=OB[:], in0=HF[:], in1=SG[:], op=ALU.mult), R=[HF, SG], W=[OB])
                    for (a_, n_, h_, o_) in pieces(0, S):
                        k.dma(GT2[h_][j * 128:(j + 1) * 128, o_:o_ + n_], OB[:, a_:a_ + n_], R=[OB], W=[GD])
                k.barrier()
                gather_rows((0, 1, 6, 7))

        if "ret" in phases:
            with ExitStack() as s3:
                wr = k.sbuf([128, 8, 1024], BF16, "wr", s3)
                th = k.sbuf([128, 4], F32, "th", s3)
                k.dma(th[:], th_d.rearrange("p a b -> p (a b)"), W=[th])
                rg = k.sbuf([128, 2, 2], F32, "rg", s3)
                k.dma(rg[:], rg_d, W=[rg])
                cmask = k.sbuf([128, 4, 128], F32, "cmask", s3)
                k.dma(cmask[:], cmask_d, W=[cmask])
                cpos = k.sbuf([128, 2, 128], F32, "cpos", s3)
                k.dma(cpos[:], cpos_d, W=[cpos])
                kpos = k.sbuf([128, 2], F32, "kpos", s3)
                k.dma(kpos[:], kpos_d, W=[kpos])
                lg = softplus_neg(k, s3, th, 4, "spt")
                k.op('vector', lambda e: e.tensor_scalar(out=lg[:], in0=lg[:], scalar1=-1.0, scalar2=None, op0=ALU.mult), R=[lg], W=[lg])
                masks = k.sbuf([128, 4, 128], F32, "masks", s3)
                qdec = k.sbuf([128, 4, 128], F32, "qdec", s3)
                kdec = k.sbuf([128, 4], F32, "kdec", s3)
                gC = k.sbuf([128, 4], F32, "gC", s3)
                for d in range(2):
                    for hd in range(2):
                        idx = d * 2 + hd
                        k.op('scalar', lambda e, d=d, idx=idx: e.activation(out=masks[:, idx, :], in_=cmask[:, 2 * d, :], func=AF.Exp, scale=lg[:, idx:idx + 1]), R=[cmask, lg], W=[masks])
                        k.op('vector', lambda e, d=d, idx=idx: e.tensor_tensor(out=masks[:, idx, :], in0=masks[:, idx, :], in1=cmask[:, 2 * d + 1, :], op=ALU.mult), R=[masks, cmask], W=[masks])
                        k.op('scalar', lambda e, d=d, idx=idx: e.activation(out=qdec[:, idx, :], in_=cpos[:, d, :], func=AF.Exp, scale=lg[:, idx:idx + 1]), R=[cpos, lg], W=[qdec])
                        k.op('scalar', lambda e, d=d, idx=idx: e.activation(out=kdec[:, idx:idx + 1], in_=kpos[:, d:d + 1], func=AF.Exp, scale=lg[:, idx:idx + 1]), R=[kpos, lg], W=[kdec])
                        k.op('scalar', lambda e, idx=idx: e.activation(out=gC[:, idx:idx + 1], in_=lg[:, idx:idx + 1], func=AF.Exp, scale=128.0), R=[lg], W=[gC])
                QT = k.sbuf([128, 2, S], BF16, "QT", s3)
                KT_ = k.sbuf([128, 2, S], BF16, "KT", s3)
                V = k.sbuf([128, NCH, 256], BF16, "Vr", s3)
                OF = k.sbuf([128, 2, S], F32, "OF", s3)
                OBk = k.sbuf([128, 2, S], F32, "OBk", s3)
                order = [list(range(NCH)), [1, 0] + list(range(NCH - 1, 1, -1))]
                for hd in range(2):
                  for i_, base_ in enumerate((QQ, KK, VV, GB)):
                      k.dma(wr[:, :, i_ * 256:(i_ + 1) * 256], win[:, :, base_ + hd * 256:base_ + hd * 256 + 256], W=[wr], q='gpsimd')
                  with ExitStack() as sA:
                    rtab = k.sbuf([128, 2, 2, 512], F32, "rtab", sA)
                    rawr = Ring([k.sbuf([128, 512], BF16, f"raw{i}", sA) for i in range(2)])
                    t1r = Ring([k.sbuf([128, 512], F32, f"t1r{i}", sA) for i in range(2)])
                    t2r = Ring([k.sbuf([128, 512], F32, f"t2r{i}", sA) for i in range(2)])
                    ksr = Ring([k.sbuf([128, 256], BF16, f"ks{i}", sA) for i in range(6)])
                    attr = Ring([k.sbuf([128, 128], BF16, f"att{i}", sA) for i in range(6)])
                    tmpr = Ring([k.sbuf([128, 2, 128], F32, f"tmp{i}", sA) for i in range(4)])
                    S32 = [k.sbuf([128, 2, 256], F32, f"S32_{d}", sA) for d in range(2)]
                    Sbf = [k.sbuf([128, 2, 256], BF16, f"Sbf_{d}", sA) for d in range(2)]
                    import os as _os
                    for ti, (s0_, N) in enumerate(TT if int(_os.environ.get('RET_STOP', '9')) > 0 else []):
                        h = load_h(ti)
                        if ti > 0:
                            k.dma(rtab[:], ropeRT_d[:, :, :, (ti - 1) * 512:ti * 512], W=[rtab])
                        for dst, cb_ in ((QT, 0), (KT_, 256)):
                            for dc in range(2):
                                p = proj(h, N, wr, cb_ + dc * 128, 128)
                                if ti == 0 or _os.environ.get('RET_NOROPE'):
                                    k.op('scalar', lambda e: e.activation(out=dst[:, dc, s0_:s0_ + N], in_=p[:, :N], func=AF.Copy), R=[p], W=[dst])
                                    continue
                                raw = rawr.get(); t1 = t1r.get(); t2 = t2r.get()
                                RM = int(_os.environ.get('ROPE_MODE', '5'))
                                k.op('scalar', lambda e: e.activation(out=raw[:], in_=p[:, :], func=AF.Copy), R=[p], W=[raw])
                                if RM >= 2:
                                    pr = ps()
                                    k.op('tensor', lambda e: e.matmul(pr[:, :], lhsT=permR[:], rhs=raw[:], start=True, stop=True), R=[permR, raw], W=[pr])
                                if RM >= 3:
                                    k.op('vector', lambda e: e.tensor_tensor(out=t1[:], in0=p[:, :], in1=rtab[:, 0, dc, :], op=ALU.mult), R=[p, rtab], W=[t1])
                                if RM >= 4:
                                    k.op('vector', lambda e: e.tensor_tensor(out=t2[:], in0=pr[:, :], in1=rtab[:, 1, dc, :], op=ALU.mult), R=[pr, rtab], W=[t2])
                                if RM >= 5:
                                    k.op('vector', lambda e: e.tensor_tensor(out=dst[:, dc, s0_:s0_ + N], in0=t1[:], in1=t2[:], op=ALU.add), R=[t1, t2], W=[dst])
                                else:
                                    k.op('scalar', lambda e: e.activation(out=dst[:, dc, s0_:s0_ + N], in_=p[:, :N], func=AF.Copy), R=[p], W=[dst])
                        for jj in range(0 if _os.environ.get('RET_NOV') else N // 128):
                            pv = ps()
                            for kc in range(8):
                                k.op('tensor', lambda e, kc=kc: e.matmul(pv[:, :256], lhsT=h[:, kc, jj * 128:(jj + 1) * 128], rhs=wr[:, kc, 512:768],
                                                                      start=(kc == 0), stop=(kc == 7)), R=[h, wr], W=[pv], inc=(kc == 7))
                            ch = (s0_ + jj * 128) // 128
                            k.op('scalar', lambda e: e.activation(out=V[:, ch, :], in_=pv[:, :256], func=AF.Copy), R=[pv], W=[V])
                    import os as _os
                    rot4 = Ring(banks[:4])
                    Dbank = [[banks[4], banks[5]], [banks[6], banks[7]]]

                    def stage1(step, d):
                        c = order[d][step]; idx = d * 2 + hd; c0 = c * 128
                        dst = (OF, OBk)[d]
                        pt = rot4.get()
                        ptb = pt[:, :].bitcast(BF16)
                        for dc in range(2):
                            k.op('tensor', lambda e, dc=dc: e.transpose(out=ptb[:, dc * 128:(dc + 1) * 128], in_=KT_[:, dc, c0:c0 + 128], identity=ident[:]), R=[KT_, ident], W=[pt])
                        ks = ksr.get()
                        k.op('vector', lambda e: e.tensor_scalar(out=ks[:], in0=ptb[:, 0:256], scalar1=kdec[:, idx:idx + 1], scalar2=None, op0=ALU.mult), R=[pt, kdec], W=[ks])
                        sT = rot4.get()
                        for dc in range(2):
                            k.op('tensor', lambda e, dc=dc: e.matmul(sT[:, :128], lhsT=KT_[:, dc, c0:c0 + 128], rhs=QT[:, dc, c0:c0 + 128], start=(dc == 0), stop=(dc == 1)), R=[KT_, QT], W=[sT], inc=(dc == 1))
                        att = attr.get()
                        k.op('vector', lambda e: e.tensor_tensor(out=att[:], in0=sT[:, :128], in1=masks[:, idx, :], op=ALU.mult), R=[sT, masks], W=[att])
                        A = rot4.get()
                        for ec in range(2):
                            k.op('tensor', lambda e, ec=ec: e.matmul(A[:, ec * 128:(ec + 1) * 128], lhsT=V[:, c, ec * 128:(ec + 1) * 128], rhs=att[:], start=True, stop=True), R=[V, att], W=[A])
                        k.op('scalar', lambda e: e.activation(out=dst[:, :, c0:c0 + 128], in_=A[:, 0:256].rearrange("p (a n) -> p a n", a=2), func=AF.Copy), R=[A], W=[dst])
                        if step < NCH - 1:
                            D = Dbank[d][step % 2]
                            for dc in range(2):
                                k.op('tensor', lambda e, dc=dc: e.matmul(D[:, dc * 256:(dc + 1) * 256], lhsT=ks[:, dc * 128:(dc + 1) * 128], rhs=V[:, c, :], start=True, stop=True), R=[ks, V], W=[D])

                    def stage2(step, d):
                        c = order[d][step]; idx = d * 2 + hd; c0 = c * 128
                        dst = (OF, OBk)[d]
                        if step > 0:
                            B = rot4.get()
                            for ec in range(2):
                                for dc in range(2):
                                    k.op('tensor', lambda e, ec=ec, dc=dc: e.matmul(B[:, ec * 128:(ec + 1) * 128], lhsT=Sbf[d][:, dc, ec * 128:(ec + 1) * 128], rhs=QT[:, dc, c0:c0 + 128],
                                                                                  start=(dc == 0), stop=(dc == 1)), R=[Sbf[d], QT], W=[B])
                            tmp = tmpr.get()
                            k.op('vector', lambda e: e.tensor_tensor(out=tmp[:], in0=B[:, 0:256].rearrange("p (a n) -> p a n", a=2),
                                                                     in1=qdec[:, idx, :].unsqueeze(1).to_broadcast([128, 2, 128]), op=ALU.mult), R=[B, qdec], W=[tmp])
                            k.op('gpsimd', lambda e: e.tensor_tensor(out=dst[:, :, c0:c0 + 128], in0=dst[:, :, c0:c0 + 128], in1=tmp[:], op=ALU.add), R=[dst, tmp], W=[dst])
                        if step < NCH - 1:
                            D = Dbank[d][step % 2]
                            Dv = D[:, :].rearrange("p (a n) -> p a n", a=2)
                            if step == 0:
                                k.op('scalar', lambda e: e.activation(out=S32[d][:], in_=Dv, func=AF.Copy), R=[D], W=[S32[d]])
                            else:
                                k.op('vector', lambda e: e.scalar_tensor_tensor(out=S32[d][:], in0=S32[d][:], scalar=gC[:, idx:idx + 1], op0=ALU.mult, in1=Dv, op1=ALU.add), R=[S32[d], D, gC], W=[S32[d]])
                            k.op('scalar', lambda e: e.activation(out=Sbf[d][:], in_=S32[d][:], func=AF.Copy), R=[S32[d]], W=[Sbf[d]])

                    for d in range(2):
                        stage1(0, d)
                    for step in range(NCH):
                        if step + 1 < NCH:
                            for d in range(2):
                                stage1(step + 1, d)
                        for d in range(2):
                            stage2(step, d)
                    k.barrier()
                  with ExitStack() as sB:
                    o_r = Ring([k.sbuf([128, 2, 512], F32, f"o_r{i}", sB) for i in range(1)])
                    ob_r = Ring([k.sbuf([128, 2, 512], BF16, f"ob_r{i}", sB) for i in range(1)])
                    sq_r = Ring([k.sbuf([128, 2, 512], BF16, f"sq_r{i}", sB) for i in range(1)])
                    st_r = Ring([k.sbuf([128, 4, 512], F32, f"st_r{i}", sB) for i in range(1)])
                    sg_r = Ring([k.sbuf([128, 512], BF16, f"sg_r{i}", sB) for i in range(2)])
                    y_r = Ring([k.sbuf([128, 512], F32, f"y_r{i}", sB) for i in range(2)])
                    oo_r = Ring([k.sbuf([128, 512], BF16, f"oo_r{i}", sB) for i in range(2)])
                    for ti, (s0_, N) in enumerate(TT if int(_os.environ.get('RET_STOP', '9')) > 2 else []):
                        h = load_h(ti)
                        o = o_r.get(); ob = ob_r.get(); sq = sq_r.get(); stt_ = st_r.get()
                        k.op('vector', lambda e: e.tensor_tensor(out=o[:, :, :N], in0=OF[:, :, s0_:s0_ + N], in1=OBk[:, :, s0_:s0_ + N], op=ALU.add), R=[OF, OBk], W=[o])
                        k.op('scalar', lambda e: e.activation(out=ob[:, :, :N], in_=o[:, :, :N], func=AF.Copy), R=[o], W=[ob])
                        k.op('scalar', lambda e: e.activation(out=sq[:, :, :N], in_=o[:, :, :N], func=AF.Square), R=[o], W=[sq])
                        p1 = ps(); p2 = ps()
                        for ec in range(2):
                            k.op('tensor', lambda e, ec=ec: e.matmul(p1[:, :N], lhsT=ones[:], rhs=ob[:, ec, :N], start=(ec == 0), stop=(ec == 1)), R=[ones, ob], W=[p1], inc=(ec == 1))
                        for ec in range(2):
                            k.op('tensor', lambda e, ec=ec: e.matmul(p2[:, :N], lhsT=ones[:], rhs=sq[:, ec, :N], start=(ec == 0), stop=(ec == 1)), R=[ones, sq], W=[p2], inc=(ec == 1))
                        mean, var, rstd, nmr = (stt_[:, i, :N] for i in range(4))
                        k.op('scalar', lambda e: e.activation(out=mean, in_=p1[:, :N], func=AF.Copy, scale=1.0 / 256.0), R=[p1], W=[stt_])
                        k.op('vector', lambda e: e.tensor_tensor(out=var, in0=mean, in1=mean, op=ALU.mult), R=[stt_], W=[stt_])
                        k.op('vector', lambda e: e.scalar_tensor_tensor(out=var, in0=p2[:, :N], scalar=1.0 / 256.0, op0=ALU.mult, in1=var, op1=ALU.subtract), R=[p2, stt_], W=[stt_])
                        k.op('scalar', lambda e: e.activation(out=rstd, in_=var, func=AF.Ln, bias=256.0 * EPS), R=[stt_], W=[stt_])
                        k.op('scalar', lambda e: e.activation(out=rstd, in_=rstd, func=AF.Exp, scale=-0.5), R=[stt_], W=[stt_])
                        k.op('vector', lambda e: e.scalar_tensor_tensor(out=nmr, in0=mean, scalar=-1.0, op0=ALU.mult, in1=rstd, op1=ALU.mult), R=[stt_], W=[stt_])
                        for ec in range(2):
                            pg = proj(h, N, wr, 768 + ec * 128, 128)
                            sg = sg_r.get(); y = y_r.get(); oo = oo_r.get()
                            k.op('scalar', lambda e: e.activation(out=sg[:, :N], in_=pg[:, :N], func=AF.Silu), R=[pg], W=[sg])
                            k.op('vector', lambda e: e.tensor_tensor(out=y[:, :N], in0=o[:, ec, :N], in1=rstd, op=ALU.mult), R=[o, stt_], W=[y])
                            k.op('vector', lambda e: e.tensor_tensor(out=y[:, :N], in0=y[:, :N], in1=nmr, op=ALU.add), R=[y, stt_], W=[y])
                            if nogate:
                                k.op('vector', lambda e: e.tensor_scalar(out=oo[:, :N], in0=y[:, :N], scalar1=rg[:, hd, ec:ec + 1], scalar2=None, op0=ALU.mult), R=[y, rg], W=[oo])
                            else:
                                k.op('vector', lambda e: e.scalar_tensor_tensor(out=oo[:, :N], in0=y[:, :N], scalar=rg[:, hd, ec:ec + 1], op0=ALU.mult, in1=sg[:, :N], op1=ALU.mult), R=[y, rg, sg], W=[oo])
                            r0_ = 512 + hd * 256 + ec * 128
                            for (a_, n_, h_, o_) in pieces(s0_, N):
                                k.dma(GT2[h_][r0_:r0_ + 128, o_:o_ + n_], oo[:, a_:a_ + n_], R=[oo], W=[GD])
                    k.barrier()
                    if hd == 1:
                        gather_rows((2, 3, 8, 9))

        if "mla" in phases:
            MSCALE = 192.0 ** -0.5
            with ExitStack() as s4:
                wqr = k.sbuf([128, 3, 4, 128], BF16, "wqr", s4)
                k.op('vector', lambda e: e.memset(wqr[:], 0.0), W=[wqr])
                for hh_ in range(4):
                    k.dma(wqr[:, :, hh_, 0:64], wqup_d[:, :, hh_ * 192 + 128:hh_ * 192 + 192], W=[wqr], q='gpsimd')
                wq = k.sbuf([128, 3, 768], BF16, "wq", s4)
                k.dma(wq[:], wqup_d, W=[wq], q='gpsimd')
                wkv = k.sbuf([128, 2, 1024], BF16, "wkv", s4)
                k.dma(wkv[:], wkvup_d, W=[wkv], q='gpsimd')
                wgc = k.sbuf([128, 8, 128], BF16, "wgc", s4)
                qn = k.sbuf([128, 3], F32, "qn", s4); k.dma(qn[:], qn_d, W=[qn])
                kvn = k.sbuf([128, 2], F32, "kvn", s4); k.dma(kvn[:], kvn_d, W=[kvn])
                g4 = k.sbuf([128, 4], F32, "g4", s4); k.dma(g4[:], gq_d, W=[g4])
                QDN = k.sbuf([128, 3, S], BF16, "QDN", s4)
                KVN = k.sbuf([128, 2, S], BF16, "KVN", s4)
                KROT = k.sbuf([128, S], BF16, "KROT", s4)
                rsr_ = Ring([k.sbuf([128, 512], F32, f"mrs{i}", s4) for i in range(2)])
                f32r = Ring([k.sbuf([128, 512], F32, f"mf{i}", s4) for i in range(4)])
                rawr_ = Ring([k.sbuf([128, 512], BF16, f"mraw{i}", s4) for i in range(2)])
                mtab = k.sbuf([128, 2, 512], F32, "mtab", s4)

                def proj_(h, N, wt, c0, M, psf):
                    p = psf()
                    for kc in range(8):
                        k.op('tensor', lambda e, kc=kc: e.matmul(p[:M, :N], lhsT=wt[:, kc, c0:c0 + M], rhs=h[:, kc, :N], start=(kc == 0), stop=(kc == 7)), R=[wt, h], W=[p], inc=(kc == 7))
                    return p

                def rmsn(pts, P, N, dim, gains, outs, out_tile, sqring, psf):
                    sq = sqring.get(); rs = rsr_.get()
                    for c, pt in enumerate(pts):
                        k.op('scalar', lambda e, c=c, pt=pt: e.activation(out=sq[:P, c, :N], in_=pt[:P, :N], func=AF.Square), R=[pt], W=[sq])
                    yield
                    pss = psf()
                    for c in range(len(pts)):
                        k.op('tensor', lambda e, c=c: e.matmul(pss[:P, :N], lhsT=ones[:P, :P], rhs=sq[:P, c, :N], start=(c == 0), stop=(c == len(pts) - 1)), R=[ones, sq], W=[pss], inc=(c == len(pts) - 1))
                    yield
                    k.op('scalar', lambda e: e.activation(out=rs[:P, :N], in_=pss[:P, :N], func=AF.Ln, scale=1.0 / dim, bias=EPS), R=[pss], W=[rs])
                    k.op('scalar', lambda e: e.activation(out=rs[:P, :N], in_=rs[:P, :N], func=AF.Exp, scale=-0.5), R=[rs], W=[rs])
                    yield
                    for c, pt in enumerate(pts):
                        k.op('vector', lambda e, c=c, pt=pt: e.scalar_tensor_tensor(out=outs[c], in0=pt[:P, :N], scalar=gains[c], op0=ALU.mult, in1=rs[:P, :N], op1=ALU.mult), R=[pt, rs] + out_tile[1:], W=[out_tile[0]])

                def rope64(src, N, out_ap, out_tile, psf):
                    raw = rawr_.get(); t1 = f32r.get(); t2 = f32r.get()
                    yield
                    k.op('scalar', lambda e: e.activation(out=raw[:, :N], in_=src[:, :N], func=AF.Copy), R=[src], W=[raw])
                    k.op('vector', lambda e: e.tensor_tensor(out=t1[:, :N], in0=src[:, :N], in1=mtab[:, 0, :N], op=ALU.mult), R=[src, mtab], W=[t1])
                    yield
                    pr = psf()
                    k.op('tensor', lambda e: e.matmul(pr[:, :N], lhsT=permM[:], rhs=raw[:, :N], start=True, stop=True), R=[permM, raw], W=[pr])
                    yield
                    k.op('vector', lambda e: e.tensor_tensor(out=t2[:, :N], in0=pr[:, :N], in1=mtab[:, 1, :N], op=ALU.mult), R=[pr, mtab], W=[t2])
                    k.op('vector', lambda e: e.tensor_tensor(out=out_ap, in0=t1[:, :N], in1=t2[:, :N], op=ALU.add), R=[t1, t2], W=[out_tile])

                def run(gen):
                    for _ in gen:
                        pass

                with ExitStack() as sC1:
                    wm_ = k.sbuf([128, 8, 768], BF16, "wmla", sC1)
                    k.op('vector', lambda e: e.memset(wm_[:], 0.0), W=[wm_])
                    k.dma(wm_[:, :, 0:704], win[:, :, QD:QD + 704], W=[wm_], q='gpsimd')
                    sq3 = Ring([k.sbuf([128, 3, 512], BF16, f"msq{i}", sC1) for i in range(2)])
                    for ti, (s0_, N) in enumerate(TT):
                        h = load_h(ti)
                        if ti > 0:
                            k.dma(mtab[:], ropeMT_d[:, :, (ti - 1) * 512:ti * 512], W=[mtab])
                        pcs = [proj_(h, N, wm_, c * 128, 128, ps) for c in range(3)]
                        run(rmsn(pcs, 128, N, 384.0, [qn[:, c:c + 1] for c in range(3)], [QDN[:, c, s0_:s0_ + N] for c in range(3)], [QDN, qn], sq3, ps))
                        pcs = [proj_(h, N, wm_, 384 + c * 128, 128, ps) for c in range(2)]
                        run(rmsn(pcs, 128, N, 256.0, [kvn[:, c:c + 1] for c in range(2)], [KVN[:, c, s0_:s0_ + N] for c in range(2)], [KVN, kvn], sq3, ps))
                        pk = proj_(h, N, wm_, 640, 128, ps)
                        if ti == 0:
                            run(rmsn([pk], 128, N, 64.0, [g4[:, 3:4]], [KROT[:, s0_:s0_ + N]], [KROT, g4], sq3, ps))
                        else:
                            kf = f32r.get()
                            run(rmsn([pk], 128, N, 64.0, [g4[:, 3:4]], [kf[:, :N]], [kf, g4], sq3, ps))
                            run(rope64(kf, N, KROT[:, s0_:s0_ + N], KROT, ps))
                    k.barrier()
                HB = [dict(KTh=k.sbuf([128, S], BF16, f"KTh{i}", s4), Vh=k.sbuf([128, NCH, 128], BF16, f"Vh{i}", s4),
                           QNh=k.sbuf([128, S], BF16, f"QNh{i}", s4), QRh=k.sbuf([128, S], BF16, f"QRh{i}", s4),
                           SGC=k.sbuf([128, S], BF16, f"SGC{i}", s4)) for i in range(2)]
                sq1 = Ring([k.sbuf([128, 1, 512], BF16, f"msq1_{i}", s4) for i in range(2)])
                pTr = Ring([k.sbuf([128, 512], BF16, f"pT{i}", s4) for i in range(4)])
                oor = Ring([k.sbuf([128, 512], BF16, f"moo{i}", s4) for i in range(2)])
                rl_a = k.sbuf([128, 512], F32, "rl_a", s4)
                y_a = k.sbuf([128, 512], F32, "y_a", s4)
                psA = Ring(banks[0:3]).get
                psP = Ring(banks[3:5]).get
                accL2 = banks[5]
                lsbA = k.sbuf([128, 512], BF16, "lsbA", s4)
                lsbB = k.sbuf([128, 512], BF16, "lsbB", s4)

                def prep(hh):
                    B_ = HB[hh % 2]
                    KTh, Vh, QNh, QRh, SGC = B_["KTh"], B_["Vh"], B_["QNh"], B_["QRh"], B_["SGC"]
                    k.dma(wgc[:], win[:, :, GC + hh * 128:GC + hh * 128 + 128], W=[wgc], q='gpsimd')
                    for ti, (s0_, N) in enumerate(TT):
                        h = load_h(ti)
                        if ti > 0:
                            k.dma(mtab[:], ropeMT_d[:, :, (ti - 1) * 512:ti * 512], W=[mtab])
                        pg = proj_(h, N, wgc, 0, 128, psP)
                        yield
                        k.op('scalar', lambda e: e.activation(out=SGC[:, s0_:s0_ + N], in_=pg[:, :N], func=AF.Silu), R=[pg], W=[SGC])
                        pkn = psP()
                        for c in range(2):
                            k.op('tensor', lambda e, c=c: e.matmul(pkn[:, :N], lhsT=wkv[:, c, hh * 256:hh * 256 + 128], rhs=KVN[:, c, s0_:s0_ + N], start=(c == 0), stop=(c == 1)), R=[wkv, KVN], W=[pkn], inc=(c == 1))
                        yield
                        yield from rmsn([pkn], 128, N, 128.0, [g4[:, 2:3]], [KTh[:, s0_:s0_ + N]], [KTh, g4], sq1, psP)
                        for jj in range(N // 128):
                            pv = psP()
                            for c in range(2):
                                k.op('tensor', lambda e, c=c: e.matmul(pv[:, :128], lhsT=KVN[:, c, s0_ + jj * 128:s0_ + (jj + 1) * 128], rhs=wkv[:, c, hh * 256 + 128:hh * 256 + 256],
                                                                     start=(c == 0), stop=(c == 1)), R=[KVN, wkv], W=[pv], inc=(c == 1))
                            yield
                            ch = (s0_ + jj * 128) // 128
                            k.op('scalar', lambda e: e.activation(out=Vh[:, ch, :], in_=pv[:, :128], func=AF.Copy), R=[pv], W=[Vh])
                        pq = psP()
                        for c in range(3):
                            k.op('tensor', lambda e, c=c: e.matmul(pq[:, :N], lhsT=wq[:, c, hh * 192:hh * 192 + 128], rhs=QDN[:, c, s0_:s0_ + N], start=(c == 0), stop=(c == 2)), R=[wq, QDN], W=[pq], inc=(c == 2))
                        yield
                        yield from rmsn([pq], 128, N, 128.0, [g4[:, 0:1]], [QNh[:, s0_:s0_ + N]], [QNh, g4], sq1, psP)
                        pqr = psP()
                        for c in range(3):
                            k.op('tensor', lambda e, c=c: e.matmul(pqr[:, :N], lhsT=wqr[:, c, hh, :], rhs=QDN[:, c, s0_:s0_ + N], start=(c == 0), stop=(c == 2)), R=[wqr, QDN], W=[pqr], inc=(c == 2))
                        yield
                        if ti == 0:
                            yield from rmsn([pqr], 128, N, 64.0, [g4[:, 1:2]], [QRh[:, s0_:s0_ + N]], [QRh, g4], sq1, psP)
                        else:
                            qf = f32r.get()
                            yield from rmsn([pqr], 128, N, 64.0, [g4[:, 1:2]], [qf[:, :N]], [qf, g4], sq1, psP)
                            yield from rope64(qf, N, QRh[:, s0_:s0_ + N], QRh, psP)
                        yield

                def attn(hh):
                    B_ = HB[hh % 2]
                    KTh, Vh, QNh, QRh, SGC = B_["KTh"], B_["Vh"], B_["QNh"], B_["QRh"], B_["SGC"]
                    for qi, (q0, N) in enumerate(TT):
                        kts = list(range(2)) if qi == 0 else list(range(NCH))

                        def score(kt):
                            sT = psA()
                            k.op('tensor', lambda e: e.matmul(sT[:, :N], lhsT=KTh[:, kt * 128:(kt + 1) * 128], rhs=QNh[:, q0:q0 + N], start=True, stop=False), R=[KTh, QNh], W=[sT], inc=False)
                            k.op('tensor', lambda e: e.matmul(sT[:, :N], lhsT=KROT[:, kt * 128:(kt + 1) * 128], rhs=QRh[:, q0:q0 + N], start=False, stop=True), R=[KROT, QRh], W=[sT])
                            return sT
                        LOOK = 2
                        pend = [score(kt) for kt in kts[:LOOK]]
                        for i_, kt in enumerate(kts):
                            if i_ + LOOK < len(kts):
                                pend.append(score(kts[i_ + LOOK]))
                            sT = pend.pop(0)
                            pT = pTr.get()
                            k.op('scalar', lambda e: e.activation(out=pT[:, :N], in_=sT[:, :N], func=AF.Exp, scale=MSCALE), R=[sT], W=[pT])
                            k.op('tensor', lambda e: e.matmul(accO[:, :N], lhsT=Vh[:, kt, :], rhs=pT[:, :N], start=(kt == kts[0]), stop=(kt == kts[-1])), R=[Vh, pT], W=[accO], inc=(kt == kts[-1]))
                            acc_ = (accL, accL2)[i_ % 2]
                            if i_ < 2:
                                k.op('vector', lambda e: e.tensor_copy(out=acc_[:, :N], in_=pT[:, :N]), R=[pT], W=[acc_])
                            else:
                                k.op('vector', lambda e: e.tensor_tensor(out=acc_[:, :N], in0=pT[:, :N], in1=acc_[:, :N], op=ALU.add), R=[pT, acc_], W=[acc_])
                            yield
                        rl = rl_a; y = y_a; oo = oor.get()
                        k.op('scalar', lambda e: e.activation(out=lsbA[:, :N], in_=accL[:, :N], func=AF.Copy), R=[accL], W=[lsbA])
                        k.op('scalar', lambda e: e.activation(out=lsbB[:, :N], in_=accL2[:, :N], func=AF.Copy), R=[accL2], W=[lsbB])
                        pl = psA()
                        k.op('tensor', lambda e: e.matmul(pl[:, :N], lhsT=ones[:], rhs=lsbA[:, :N], start=True, stop=False), R=[ones, lsbA], W=[pl], inc=False)
                        k.op('tensor', lambda e: e.matmul(pl[:, :N], lhsT=ones[:], rhs=lsbB[:, :N], start=False, stop=True), R=[ones, lsbB], W=[pl])
                        k.op('scalar', lambda e: e.activation(out=rl[:, :N], in_=pl[:, :N], func=AF.Ln), R=[pl], W=[rl])
                        k.op('scalar', lambda e: e.activation(out=rl[:, :N], in_=rl[:, :N], func=AF.Exp, scale=-1.0), R=[rl], W=[rl])
                        k.op('vector', lambda e: e.tensor_tensor(out=y[:, :N], in0=accO[:, :N], in1=rl[:, :N], op=ALU.mult), R=[accO, rl], W=[y])
                        if nogate:
                            k.op('vector', lambda e: e.tensor_copy(out=oo[:, :N], in_=y[:, :N]), R=[y], W=[oo])
                        else:
                            k.op('vector', lambda e: e.tensor_tensor(out=oo[:, :N], in0=y[:, :N], in1=SGC[:, q0:q0 + N], op=ALU.mult), R=[y, SGC], W=[oo])
                        r0_ = 1024 + hh * 128
                        for (a_, n_, h_, o_) in pieces(q0, N):
                            k.dma(GT2[h_][r0_:r0_ + 128, o_:o_ + n_], oo[:, a_:a_ + n_], R=[oo], W=[GD])
                        yield

                run(prep(0))
                for hh in range(4):
                    nxt = prep(hh + 1) if hh < 3 else None
                    for _ in attn(hh):
                        if nxt is not None:
                            try:
                                next(nxt)
                            except StopIteration:
                                nxt = None
                    if nxt is not None:
                        run(nxt)
                k.barrier()
                gather_rows((4, 5, 10, 11))

    return mod, GTl


def _consts():
    c = {}
    c["c_ident"] = np.eye(128, dtype=np.float32)
    pr = np.zeros((128, 128), np.float32)
    for j in range(64):
        pr[j + 64, j] = -1.0
        pr[j, j + 64] = 1.0
    c["c_permR"] = pr
    pm = np.zeros((128, 128), np.float32)
    for j in range(16):
        pm[j + 16, j] = -1.0
        pm[j, j + 16] = 1.0
        pm[j + 48, j + 32] = -1.0
        pm[j + 32, j + 48] = 1.0
    c["c_permM"] = pm
    pos = np.arange(64, dtype=np.float32)
    invR = (np.float32(10000.0) ** (-np.arange(64, dtype=np.float32) / np.float32(64))).astype(np.float32)
    angR = (pos[None, :] * invR[np.arange(128) % 64][:, None]).astype(np.float32)
    c["c_ropeR"] = np.stack([np.cos(angR), np.sin(angR)], axis=1).astype(np.float32)
    invM = (np.float32(10000.0) ** (-np.arange(16, dtype=np.float32) / np.float32(16))).astype(np.float32)
    angM = (pos[None, :] * invM[np.arange(64) % 16][:, None]).astype(np.float32)
    c["c_ropeM"] = np.stack([np.cos(angM), np.sin(angM)], axis=1).astype(np.float32)
    tt = np.arange(4096)
    rowi, coli = tt // 64, tt % 64
    rt = np.zeros((128, 2, 2, 4096), np.float32)
    rt[:, :, 0, :] = c["c_ropeR"][:, :, rowi]
    rt[:, :, 1, :] = c["c_ropeR"][:, :, coli]
    c["c_ropeRT"] = rt
    mt = np.zeros((128, 2, 4096), np.float32)
    mt[:32] = c["c_ropeM"][:32][:, :, rowi]
    mt[32:64] = c["c_ropeM"][32:][:, :, coli]
    c["c_ropeMT"] = mt
    m = np.arange(128)[:, None]
    n = np.arange(128)[None, :]
    mk = np.zeros((128, 4, 128), np.float32)
    mk[:, 0] = np.maximum(n - m, 0)
    mk[:, 1] = (n >= m)
    mk[:, 2] = np.maximum(m - n, 0)
    mk[:, 3] = (m >= n)
    c["c_mask"] = mk
    cp = np.zeros((128, 2, 128), np.float32)
    cp[:, 0, :] = np.arange(128) + 1
    cp[:, 1, :] = 128 - np.arange(128)
    c["c_pos"] = cp
    kp = np.zeros((128, 2), np.float32)
    kp[:, 0] = 127 - np.arange(128)
    kp[:, 1] = np.arange(128)
    c["c_kpos"] = kp
    return c


def _pc(v, nchunk):
    return np.ascontiguousarray(np.asarray(v).reshape(nchunk, 128).T)


def a_inputs(inp, l, xT_all, consts):
    maps = []
    w_in = inp["w_in"][l]
    for core in range(8):
        b, p = core // 2, core % 2
        m = dict(consts)
        m["xT"] = np.ascontiguousarray(xT_all[b])
        cc = np.stack([inp["c"][b], inp["c_ctx"]], axis=-1)
        m["cT"] = np.ascontiguousarray(cc.reshape(8, 128, 2).transpose(1, 0, 2))
        m["wmod"] = inp["w_mod"][l]
        m["bmod"] = _pc(inp["b_mod"][l], 24)
        m["ngain"] = _pc(inp["norm_gain"][l], 8)
        cols = np.concatenate([
            np.arange(0 + p * 512, 0 + p * 512 + 512), np.arange(1024 + p * 512, 1024 + p * 512 + 512),
            np.arange(2048 + p * 512, 2048 + p * 512 + 512), np.arange(3072 + p * 512, 3072 + p * 512 + 512),
            np.arange(4096 + p * 512, 4096 + p * 512 + 512), np.arange(5120 + p * 512, 5120 + p * 512 + 512),
            np.arange(6144, 6848), np.arange(6848 + p * 512, 6848 + p * 512 + 512)])
        m["win"] = np.ascontiguousarray(w_in[:, cols])
        sl = slice(p * 512, (p + 1) * 512)
        m["convw"] = np.ascontiguousarray(inp["conv_w"][l][:, sl].reshape(4, 4, 128).transpose(2, 1, 0))
        m["convb"] = _pc(inp["conv_b"][l][sl], 4)
        m["lru_wa"] = np.ascontiguousarray(inp["lru_wa"][l][:, 4 * p:4 * p + 4].transpose(2, 0, 1, 3))
        m["lru_wx"] = np.ascontiguousarray(inp["lru_wx"][l][:, 4 * p:4 * p + 4].transpose(2, 0, 1, 3))
        for nm in ("lru_ba", "lru_bx"):
            m[nm] = np.ascontiguousarray(inp[nm][l][:, sl].reshape(2, 4, 128).transpose(2, 0, 1))
        m["lru_lam"] = np.ascontiguousarray(inp["lru_lambda"][l][:, sl].reshape(2, 4, 128).transpose(2, 0, 1))
        m["ret_theta"] = np.ascontiguousarray(np.broadcast_to(inp["ret_theta"][l][:, 2 * p:2 * p + 2][None], (128, 2, 2)))
        m["ret_gain"] = np.ascontiguousarray(inp["ret_gain"][l][sl].reshape(2, 2, 128).transpose(2, 0, 1))
        m["mla_qnorm"] = _pc(inp["mla_q_norm"][l], 3)
        m["mla_wqup"] = np.ascontiguousarray(inp["mla_w_q_up"][l][:, p * 768:(p + 1) * 768])
        m["mla_kvnorm"] = _pc(inp["mla_kv_norm"][l], 2)
        m["mla_wkvup"] = np.ascontiguousarray(inp["mla_w_kv_up"][l][:, p * 1024:(p + 1) * 1024])
        g4 = np.zeros((128, 4), np.float32)
        g4[:, 0] = inp["mla_g_qn"][l]
        g4[:64, 1] = inp["mla_g_qr"][l]
        g4[:, 2] = inp["mla_g_kn"][l]
        g4[:64, 3] = inp["mla_g_kr"][l]
        m["mla_g4"] = g4
        maps.append({k_: np.ascontiguousarray(v, dtype=np.float32) for k_, v in m.items()})
    return maps


def emit_B(nc, k, l, din, dint, dout, SH, mod, GTl):
    banks = SH["banks"]
    GD, GALL = SH["GD"], SH["GALL"]
    GTall = dint("GTall", [2 * 3072, NTB], BF16)
    Gm = dint("Gmine", [3072, NTB], BF16)
    hm = dint("hmine", [1024, NTB], BF16)
    hT2 = dint("hT2", [2 * 1024, NTB], BF16)
    GM, HM = SH["GM"], SH["HM"]
    Gv = [Gm[r_ * 1536:(r_ + 1) * 1536, :].rearrange("(c p) t -> p c t", p=128) for r_ in range(2)]
    hv = hm.rearrange("(c p) t -> p c t", p=128)
    x1s = dint("x1s", [1024, NTB], F32)
    if l == 0:
        xsrc = din("xTs", [1024, NTB]).rearrange("(c p) t -> p c t", p=128)
        xdst = x1s.rearrange("(c p) t -> p c t", p=128)
    else:
        xsrc = x1s.rearrange("(c p) t -> p c t", p=128)
        xdst = dout("xo", [1024, NTB], F32).rearrange("(c p) t -> p c t", p=128)
    wgm_d = din(f"wgm_{l}", [1024, 3072]).rearrange("(c p) n -> p c n", p=128)
    wd_d = [din(f"{nm}_{l}", [1024, 1024]).rearrange("(c p) n -> p c n", p=128) for nm in ("wda", "wdb", "wdc", "wout")]
    HD = SH["HDall"]
    XS, XO = SH["XS"], SH["XO"]
    with ExitStack() as sb:
        ps = Ring(banks).get
        wgm = k.sbuf([128, 8, 3072], BF16, "wgm", sb)
        for kc in range(8):
            k.dma(wgm[:, kc, :], wgm_d[:, kc, :], W=[wgm], q='gpsimd')
        wd = [k.sbuf([128, 8, 1024], BF16, f"wd{i}", sb) for i in range(4)]
        for i in range(4):
            for kc in range(0, 8, 4):
                k.dma(wd[i][:, kc:kc + 4, :], wd_d[i][:, kc:kc + 4, :], W=[wd[i]], q='gpsimd')
        def exchange():
            CH = 256
            GTall = dint("GTall", [2 * 3072, NTB], BF16)
            half = nc.sync.partition_id() % 2
            Gm = dint("Gmine", [3072, NTB], BF16)
            hm = dint("hmine", [1024, NTB], BF16)
            hT2 = dint("hT2", [2 * 1024, NTB], BF16)
            GM, HM = SH["GM"], SH["HM"]
            GTall4 = GTall.rearrange("(c r q) t -> c r q t", r=2, q=CH)
            for r_ in range(2):
                src = GTall4[bass.ds(half * 6, 6), r_, :, :]
                k.dma(Gm[r_ * 1536:(r_ + 1) * 1536, :].rearrange("(c q) t -> c q t", q=CH), src, R=[GALL], W=[GM])
            k.dma(hm[:, :], hT2[bass.ds(half * 1024, 1024), :], R=SH["HDall"], W=[HM])
        exchange()
        Gr = Ring([k.sbuf([128, 24, 256], BF16, f"G{i}", sb) for i in range(2)])
        hr = Ring([k.sbuf([128, 8, 256], BF16, f"h{i}", sb) for i in range(2)])
        xr = Ring([k.sbuf([128, 8, 256], F32, f"x{i}", sb) for i in range(1)])
        xor_ = Ring([k.sbuf([128, 8, 256], F32, f"xo{i}", sb) for i in range(1)])
        m = k.sbuf([128, 24, 256], BF16, "m", sb)
        z = k.sbuf([128, 8, 256], BF16, "z", sb)
        t1r = Ring([k.sbuf([128, 256], F32, f"t1_{i}", sb) for i in range(2)])
        t2r = Ring([k.sbuf([128, 256], F32, f"t2_{i}", sb) for i in range(2)])
        for ti, (s0_, N) in enumerate(TB):
            col = 1 if ti == 0 else 0
            G = Gr.get(); h = hr.get(); x = xr.get(); xo = xor_.get()
            for r_ in range(2):
                k.dma(G[:, r_ * 12:(r_ + 1) * 12, :N], Gv[r_][:, :, s0_:s0_ + N], R=[GM], W=[G])
            k.dma(h[:, :, :N], hv[:, :, s0_:s0_ + N], R=[HM], W=[h])
            k.dma(x[:, :, :N], xsrc[:, :, s0_:s0_ + N], R=[XS], W=[x])
            for j in range(24):
                p = ps()
                for kc in range(8):
                    k.op('tensor', lambda e, kc=kc: e.matmul(p[:, :N], lhsT=wgm[:, kc, j * 128:(j + 1) * 128], rhs=h[:, kc, :N], start=(kc == 0), stop=(kc == 7)), R=[wgm, h], W=[p], inc=(kc == 7))
                k.op('scalar', lambda e: e.activation(out=m[:, j, :N], in_=p[:, :N], func=AF.Sigmoid), R=[p], W=[m])
            for oc in range(8):
                pp = [ps(), ps(), ps()]
                for br in range(3):
                    for kc in range(8):
                        gi = (kc // 4) * 12 + br * 4 + (kc % 4)
                        k.op('tensor', lambda e, br=br, kc=kc, gi=gi: e.matmul(pp[br][:, :N], lhsT=wd[br][:, kc, oc * 128:(oc + 1) * 128], rhs=G[:, gi, :N], start=(kc == 0), stop=(kc == 7)),
                             R=[wd[br], G], W=[pp[br]])
                t1 = t1r.get(); t2 = t2r.get()
                k.op('vector', lambda e: e.tensor_tensor(out=t1[:, :N], in0=pp[0][:, :N], in1=m[:, oc, :N], op=ALU.mult), R=[pp[0], m], W=[t1])
                k.op('vector', lambda e: e.tensor_tensor(out=t2[:, :N], in0=pp[1][:, :N], in1=m[:, 8 + oc, :N], op=ALU.mult), R=[pp[1], m], W=[t2])
                k.op('vector', lambda e: e.tensor_tensor(out=t1[:, :N], in0=t1[:, :N], in1=t2[:, :N], op=ALU.add), R=[t1, t2], W=[t1])
                k.op('vector', lambda e: e.tensor_tensor(out=t2[:, :N], in0=pp[2][:, :N], in1=m[:, 16 + oc, :N], op=ALU.mult), R=[pp[2], m], W=[t2])
                k.op('vector', lambda e: e.tensor_tensor(out=z[:, oc, :N], in0=t1[:, :N], in1=t2[:, :N], op=ALU.add), R=[t1, t2], W=[z])
            for oc in range(8):
                p = ps()
                for kc in range(8):
                    k.op('tensor', lambda e, kc=kc: e.matmul(p[:, :N], lhsT=wd[3][:, kc, oc * 128:(oc + 1) * 128], rhs=z[:, kc, :N], start=(kc == 0), stop=(kc == 7)), R=[wd[3], z], W=[p])
                k.op('vector', lambda e: e.scalar_tensor_tensor(out=xo[:, oc, :N], in0=p[:, :N], scalar=mod[:, 16 + oc, col:col + 1], op0=ALU.mult, in1=x[:, oc, :N], op1=ALU.add), R=[p, mod, x], W=[xo])
            k.dma(xdst[:, :, s0_:s0_ + N], xo[:, :, :N], R=[xo], W=[XO if l == 1 else XS])
        k.barrier()
    if l == 0:
        x1all = dint("x1all", [2 * 1024, NTB], F32)
        for c_ in range(8):
            k.collective("AllGather", PAIRS, x1s[c_ * 128:(c_ + 1) * 128, :], x1all[c_ * 256:(c_ + 1) * 256, :], R=[XS], W=[SH["X1ALL"]])


PAIRS = [[0, 1], [2, 3], [4, 5], [6, 7]]
_DECLARED = set()


def build_F(upto=None):
    nc = bass.Bass("TRN2", target_bir_lowering=False)
    dram = {}

    def din(name, shape, dt=F32):
        if name not in dram:
            dram[name] = nc.dram_tensor(name, list(shape), dt, kind="ExternalInput").ap()
            _DECLARED.add(name)
        return dram[name]

    def dint(name, shape, dt):
        if name not in dram:
            dram[name] = nc.dram_tensor(name, list(shape), dt, kind="Internal").ap()
        return dram[name]

    def dout(name, shape, dt):
        if name not in dram:
            dram[name] = nc.dram_tensor(name, list(shape), dt, kind="ExternalOutput").ap()
        return dram[name]

    with ExitStack() as st:
        k = K(nc, st)
        SH = {}
        SH["banks"] = [k.psum([128, 512], F32, f"ps{i}") for i in range(8)]
        ones = k.sbuf([128, 128], BF16, "ones")
        k.op('vector', lambda e: e.memset(ones[:], 1.0), W=[ones])
        SH["ones"] = ones
        for nm in ("ident", "permR", "permM"):
            t = k.sbuf([128, 128], BF16, nm)
            k.dma(t[:], din("c_" + nm, [128, 128]), W=[t], q='gpsimd')
            SH[nm] = t
        SH["hring"] = Ring([k.sbuf([128, 8, 512], BF16, f"hr{i}") for i in range(2)])
        for nm in ("GD", "GALL", "X1ALL", "XS", "XO", "GM", "HM"):
            SH[nm] = k.fake(nm)
        stop = False
        for l in range(2):
            SH["HDall"] = []
            mod, GTl = emit_A(nc, k, l, din, dint, SH)
            if upto == f"A{l}":
                stop = True
                break
            if upto == f"A{l}c":
                GTall = dint("GTall", [2 * 3072, NTB], BF16)
                k.barrier()
                stop = True
                break
            emit_B(nc, k, l, din, dint, dout, SH, mod, GTl)
            if upto == f"B{l}":
                stop = True
                break
        if stop and "xo" not in dram:
            xo_ = dout("xo", [1024, NTB], F32)
            k.barrier()
            k.dma(xo_[0:128, 0:48], mod[:].rearrange("p a b -> p (a b)"), R=[mod], W=[SH["XO"]])
        k.finish([SH["XO"]])
        print("F: ops", k.n_ops, "sems", k.nsem)
    return nc


def _tok_idx(p):
    return np.concatenate([np.arange(p * 128, (p + 1) * 128), 256 + np.arange(p * 2048, (p + 1) * 2048)])


def kernel(**inp):
    inp = {k_: np.asarray(v) for k_, v in inp.items()}
    consts = _consts()
    xT_all = [np.ascontiguousarray(np.concatenate([inp["ctx"][b], inp["x"][b]], 0).T) for b in range(4)]
    per_layer = [a_inputs(inp, l, xT_all, consts) for l in range(2)]
    maps = []
    for core in range(8):
        b, p = core // 2, core % 2
        m = {}
        for l in range(2):
            for k_, v in per_layer[l][core].items():
                if k_.startswith("c_") or k_ == "xT":
                    m[k_] = v
                else:
                    m[f"{k_}_{l}"] = v
            m[f"wgm_{l}"] = np.ascontiguousarray(inp["w_in"][l][:, 7872:10944])
            m[f"wda_{l}"] = inp["w_down_a"][l]
            m[f"wdb_{l}"] = inp["w_down_b"][l]
            m[f"wdc_{l}"] = inp["w_down_c"][l]
            m[f"wout_{l}"] = inp["w_out"][l]
        m["xTs"] = np.ascontiguousarray(xT_all[b][:, _tok_idx(p)])
        maps.append(m)
    nc = build_F()
    used = set(t for t in _DECLARED)
    maps = [{k_: v for k_, v in m.items() if k_ in used} for m in maps]
    res = run_bass_kernel_spmd(nc, maps, core_ids=list(range(8))).results
    out = np.empty((4, 4096, 1024), np.float32)
    for core in range(8):
        b, p = core // 2, core % 2
        out[b, p * 2048:(p + 1) * 2048, :] = res[core]["xo"][:, 128:].T
    return out
```

```python
import numpy as np
from contextlib import ExitStack
import ml_dtypes
import concourse.bass as bass
import concourse.mybir as mybir
from concourse.bass_utils import run_bass_kernel_spmd

F32 = mybir.dt.float32
BF16 = mybir.dt.bfloat16
AF = mybir.ActivationFunctionType
ALU = mybir.AluOpType

S = 4352
L = 256
TT = [(0, 256)] + [(256 + 512 * i, 512) for i in range(8)]
NCH = 34
EPS = 1e-6
XA, GA, QQ, KK, VV, GB, QD, KVD, KR, GC = 0, 512, 1024, 1536, 2048, 2560, 3072, 3456, 3712, 3776
NWIN = 4288
NTB = 2176
TB = [(0, 128)] + [(128 + 256 * i, 256) for i in range(8)]


class Sem:
    def __init__(self, h, name):
        self.h = h
        self.name = name
        self.val = 0


class Tile:
    def __init__(self, t, name):
        self.t = t
        self.name = name
        self.w = None
        self.r = {}
        self.dsem = None

    def __getitem__(self, k):
        return self.t[k]


class Eng:
    def __init__(self, name, eng, sem):
        self.name = name
        self.e = eng
        self.sem = sem
        self.seen = {}


class K:
    def __init__(self, nc, stack):
        self.nc = nc
        self.stack = stack
        self.engs = {}
        self.nsem = 0
        self.n_ops = 0
        self.sems = []
        for n in ['tensor', 'vector', 'scalar', 'gpsimd', 'sync']:
            self.engs[n] = Eng(n, getattr(nc, n), self.new_sem(n))

    def new_sem(self, name):
        self.nsem += 1
        assert self.nsem < 140, "too many semaphores"
        sm = Sem(self.stack.enter_context(self.nc.semaphore(f"s{self.nsem}_{name}")), name)
        self.sems.append(sm)
        return sm

    def barrier(self):
        for eng in self.engs.values():
            for s in self.sems:
                if s.val > 0 and eng.seen.get(s, 0) < s.val and not (s is eng.sem):
                    eng.e.wait_ge(s.h, s.val)
                    eng.seen[s] = s.val

    def sbuf(self, shape, dtype, name, stack=None):
        st = stack if stack is not None else self.stack
        self.n_t = getattr(self, "n_t", 0) + 1
        t = Tile(st.enter_context(self.nc.sbuf_tensor(f"{name}_{self.n_t}", list(shape), dtype)), name)
        t.base = name
        return t

    def psum(self, shape, dtype, name):
        t = Tile(self.stack.enter_context(self.nc.psum_tensor(name, list(shape), dtype)), name)
        t.excl = True
        return t

    def fake(self, name):
        t = Tile(None, name)
        t.base = name
        return t

    def collective(self, kind, groups, in_ap, out_ap, R=(), W=()):
        eng = self.engs['gpsimd']
        self._waits(eng, R, W)
        if not hasattr(self, "ccsem"):
            self.ccsem = self.new_sem('cc')
        s_ = self.ccsem
        inst = self.nc.gpsimd.collective_compute(kind, ALU.bypass, groups, ins=[in_ap], outs=[out_ap])
        s_.val += 1
        inst.then_inc(s_.h)
        for t in W:
            t.w = (s_, s_.val)
            t.r = {}
        for t in R:
            t.r[s_] = s_.val

    def _waits(self, eng, R, W):
        need = {}

        def add(s, v):
            if need.get(s, 0) < v:
                need[s] = v
        for t in R:
            if t.w is not None:
                add(*t.w)
        for t in W:
            if t.w is not None:
                add(*t.w)
            for s, v in t.r.items():
                add(s, v)
        for s, v in need.items():
            if s is eng.sem and eng.name == 'tensor':
                continue
            if eng.seen.get(s, 0) >= v:
                continue
            eng.e.wait_ge(s.h, v)
            eng.seen[s] = v

    def op(self, engname, fn, R=(), W=(), inc=True):
        eng = self.engs[engname]
        W = list(W) + [t for t in R if getattr(t, "excl", False) and t not in W]
        self._waits(eng, R, W)
        inst = fn(eng.e)
        if inc:
            eng.sem.val += 1
            inst.then_inc(eng.sem.h, 1)
            v = eng.sem.val
        else:
            assert engname == 'tensor'
            v = eng.sem.val + 1
        for t in W:
            t.w = (eng.sem, v)
            t.r = {}
        for t in R:
            if t in W:
                continue
            t.r[eng.sem] = v
        self.n_ops += 1
        return inst

    def dma(self, out, in_, R=(), W=(), q='sync'):
        eng = self.engs[q]
        self._waits(eng, R, W)
        owner = W[0] if len(W) else R[0]
        if owner.dsem is None:
            if not hasattr(self, "dsems"):
                self.dsems = {}
            base = getattr(owner, "base", owner.name)
            if base not in self.dsems:
                self.dsems[base] = self.new_sem('d_' + base)
            owner.dsem = self.dsems[base]
        s = owner.dsem
        inst = eng.e.dma_start(out=out, in_=in_)
        s.val += 16
        inst.then_inc(s.h, 16)
        for t in W:
            t.w = (s, s.val)
            t.r = {}
        for t in R:
            t.r[s] = s.val
        return inst

    def finish(self, tiles):
        eng = self.engs['sync']
        for t in tiles:
            self._waits(eng, [t], [t])


def rev_ap(ap):
    dims = [list(d) for d in ap.ap]
    step, n = dims[-1]
    dims[-1] = [-step, n]
    return bass.AP(ap.tensor, ap.offset + (n - 1) * step, dims)


class Ring:
    def __init__(self, tiles):
        self.tiles = tiles
        self.i = 0

    def get(self):
        t = self.tiles[self.i % len(self.tiles)]
        self.i += 1
        return t


def softplus_neg(k, stack, src, n, name):
    x = k.sbuf([128, n], F32, name + "_x", stack)
    y = k.sbuf([128, n], F32, name + "_y", stack)
    y2 = k.sbuf([128, n], F32, name + "_y2", stack)
    pl = k.sbuf([128, n], F32, name + "_p", stack)
    k.op('scalar', lambda e: e.activation(out=x[:], in_=src[:], func=AF.Exp, scale=-1.0), R=[src], W=[x])
    k.op('vector', lambda e: e.tensor_scalar(out=y[:], in0=x[:], scalar1=2.0, scalar2=None, op0=ALU.add), R=[x], W=[y])
    k.op('vector', lambda e: e.reciprocal(out=y[:], in_=y[:]), R=[y], W=[y])
    k.op('vector', lambda e: e.tensor_tensor(out=y[:], in0=y[:], in1=x[:], op=ALU.mult), R=[y, x], W=[y])
    k.op('vector', lambda e: e.tensor_tensor(out=y2[:], in0=y[:], in1=y[:], op=ALU.mult), R=[y], W=[y2])
    k.op('vector', lambda e: e.tensor_scalar(out=pl[:], in0=y2[:], scalar1=1.0 / 11.0, scalar2=1.0 / 9.0, op0=ALU.mult, op1=ALU.add), R=[y2], W=[pl])
    for cst in (1.0 / 7.0, 1.0 / 5.0, 1.0 / 3.0, 1.0):
        k.op('vector', lambda e: e.tensor_tensor(out=pl[:], in0=pl[:], in1=y2[:], op=ALU.mult), R=[pl, y2], W=[pl])
        k.op('vector', lambda e, cst=cst: e.tensor_scalar(out=pl[:], in0=pl[:], scalar1=cst, scalar2=None, op0=ALU.add), R=[pl], W=[pl])
    k.op('vector', lambda e: e.tensor_tensor(out=pl[:], in0=pl[:], in1=y[:], op=ALU.mult), R=[pl, y], W=[pl])
    k.op('vector', lambda e: e.tensor_scalar(out=pl[:], in0=pl[:], scalar1=2.0, scalar2=None, op0=ALU.mult), R=[pl], W=[pl])
    return pl


def emit_A(nc, k, l, din, dint, SH, phases=("lru", "ret", "mla"), nogate=False):
    xT = din("xT", [1024, S]).rearrange("(c p) t -> p c t", p=128)
    cT = din(f"cT_{l}", [128, 8, 2])
    wmod = din(f"wmod_{l}", [1024, 3072]).rearrange("(c p) n -> p c n", p=128)
    bmod = din(f"bmod_{l}", [128, 24])
    ngain = din(f"ngain_{l}", [128, 8])
    win = din(f"win_{l}", [1024, NWIN]).rearrange("(c p) n -> p c n", p=128)
    cw_d = din(f"convw_{l}", [128, 4, 4])
    cb_d = din(f"convb_{l}", [128, 4])
    wa_d = din(f"lru_wa_{l}", [128, 2, 4, 128])
    wx_d = din(f"lru_wx_{l}", [128, 2, 4, 128])
    ba_d = din(f"lru_ba_{l}", [128, 2, 4])
    bx_d = din(f"lru_bx_{l}", [128, 2, 4])
    lam_d = din(f"lru_lam_{l}", [128, 2, 4])
    th_d = din(f"ret_theta_{l}", [128, 2, 2])
    rg_d = din(f"ret_gain_{l}", [128, 2, 2])
    qn_d = din(f"mla_qnorm_{l}", [128, 3])
    wqup_d = din(f"mla_wqup_{l}", [384, 768]).rearrange("(c p) n -> p c n", p=128)
    kvn_d = din(f"mla_kvnorm_{l}", [128, 2])
    wkvup_d = din(f"mla_wkvup_{l}", [256, 1024]).rearrange("(c p) n -> p c n", p=128)
    gq_d = din(f"mla_g4_{l}", [128, 4])
    ident_d = din("c_ident", [128, 128])
    permR_d = din("c_permR", [128, 128])
    permM_d = din("c_permM", [128, 128])
    ropeR_d = din("c_ropeR", [128, 2, 64])
    ropeM_d = din("c_ropeM", [64, 2, 64])
    ropeRT_d = din("c_ropeRT", [128, 2, 2, 4096])
    ropeMT_d = din("c_ropeMT", [128, 2, 4096])
    cmask_d = din("c_mask", [128, 4, 128])
    cpos_d = din("c_pos", [128, 2, 128])
    kpos_d = din("c_kpos", [128, 2])

    GTl = dint("GTl", [2 * 1536, NTB], BF16)
    GT2 = [GTl[h_ * 1536:(h_ + 1) * 1536, :] for h_ in range(2)]
    hT2 = dint("hT2", [2 * 1024, NTB], BF16)
    hT2v = [hT2[h_ * 1024:(h_ + 1) * 1024, :].rearrange("(c p) t -> p c t", p=128) for h_ in range(2)]
    x1all = dint("x1all", [2 * 1024, NTB], F32)
    x1v = [x1all.rearrange("(c h p) t -> h p c t", h=2, p=128)[h_] for h_ in range(2)]

    def pieces(s0_, N):
        out_ = []
        for c0 in range(s0_, s0_ + N, 128):
            if c0 < L:
                out_.append((c0 - s0_, 128, c0 // 128, 0))
            else:
                t_ = c0 - L
                out_.append((c0 - s0_, 128, t_ // 2048, 128 + t_ % 2048))
        mg = [list(out_[0])]
        for (a_, n_, h_, o_) in out_[1:]:
            if h_ == mg[-1][2] and o_ == mg[-1][3] + mg[-1][1] and a_ == mg[-1][0] + mg[-1][1]:
                mg[-1][1] += n_
            else:
                mg.append([a_, n_, h_, o_])
        return [tuple(m_) for m_ in mg]

    if True:
        banks = SH["banks"]
        rot = Ring(banks[:6])
        ps = rot.get
        accO, accL = banks[6], banks[7]
        ones, ident, permR, permM = SH["ones"], SH["ident"], SH["permR"], SH["permM"]

        mod = k.sbuf([128, 24, 2], F32, "mod")
        gs = k.sbuf([128, 8, 2], F32, "gs")
        with ExitStack() as s0:
            wm = k.sbuf([128, 8, 3072], BF16, "wm", s0)
            for kc in range(8):
                k.dma(wm[:, kc, :], wmod[:, kc, :], W=[wm], q='gpsimd')
            ct = k.sbuf([128, 8, 2], F32, "ct", s0)
            cb = k.sbuf([128, 8, 2], BF16, "cb", s0)
            bm = k.sbuf([128, 24], F32, "bm", s0)
            ng = k.sbuf([128, 8], F32, "ng", s0)
            k.dma(ct[:], cT, W=[ct])
            k.dma(bm[:], bmod, W=[bm])
            k.dma(ng[:], ngain, W=[ng])
            k.op('scalar', lambda e: e.activation(out=cb[:], in_=ct[:], func=AF.Silu), R=[ct], W=[cb])
            pm = ps()
            for j in range(24):
                for kc in range(8):
                    k.op('tensor', lambda e, j=j, kc=kc: e.matmul(pm[:, 2 * j:2 * j + 2], lhsT=wm[:, kc, j * 128:(j + 1) * 128], rhs=cb[:, kc, :],
                                                                 start=(kc == 0), stop=(kc == 7)), R=[wm, cb], W=[pm], inc=(kc == 7))
            k.op('vector', lambda e: e.tensor_tensor(out=mod[:], in0=pm[:, 0:48].rearrange("p (j t) -> p j t", t=2),
                                                     in1=bm[:].unsqueeze(2).to_broadcast([128, 24, 2]), op=ALU.add), R=[pm, bm], W=[mod])
            k.op('vector', lambda e: e.tensor_scalar(out=gs[:], in0=mod[:, 8:16, :], scalar1=1.0, scalar2=None, op0=ALU.add), R=[mod], W=[gs])
            k.op('vector', lambda e: e.tensor_tensor(out=gs[:], in0=gs[:], in1=ng[:].unsqueeze(2).to_broadcast([128, 8, 2]), op=ALU.mult), R=[gs, ng], W=[gs])
            k.barrier()

        hD = [k.fake(f"hD{i}") for i in range(len(TT))]
        SH["HDall"] = hD
        with ExitStack() as s1:
            xr = Ring([k.sbuf([128, 8, 512], F32, f"xr{i}", s1) for i in range(2)])
            sqr = Ring([k.sbuf([128, 8, 512], BF16, f"sq{i}", s1) for i in range(2)])
            hor = Ring([k.sbuf([128, 8, 512], BF16, f"ho{i}", s1) for i in range(2)])
            rsr = Ring([k.sbuf([128, 512], F32, f"rs{i}", s1) for i in range(2)])
            for ti, (s0_, N) in enumerate(TT):
                col = 1 if ti == 0 else 0
                xt = xr.get(); sq = sqr.get(); ho = hor.get(); rs = rsr.get()
                if l == 0:
                    k.dma(xt[:, :, :N], xT[:, :, s0_:s0_ + N], W=[xt])
                else:
                    for (a_, n_, h_, o_) in pieces(s0_, N):
                        k.dma(xt[:, :, a_:a_ + n_], x1v[h_][:, :, o_:o_ + n_], R=[SH["X1ALL"]], W=[xt])
                k.op('scalar', lambda e: e.activation(out=sq[:, :, :N], in_=xt[:, :, :N], func=AF.Square), R=[xt], W=[sq])
                p = ps()
                for c in range(8):
                    k.op('tensor', lambda e, c=c: e.matmul(p[:, :N], lhsT=ones[:], rhs=sq[:, c, :N], start=(c == 0), stop=(c == 7)), R=[ones, sq], W=[p], inc=(c == 7))
                k.op('scalar', lambda e: e.activation(out=rs[:, :N], in_=p[:, :N], func=AF.Ln, scale=1.0 / 1024.0, bias=EPS), R=[p], W=[rs])
                k.op('scalar', lambda e: e.activation(out=rs[:, :N], in_=rs[:, :N], func=AF.Exp, scale=-0.5), R=[rs], W=[rs])
                k.op('vector', lambda e: e.tensor_tensor(out=xt[:, :, :N], in0=xt[:, :, :N], in1=rs[:, :N].unsqueeze(1).to_broadcast([128, 8, N]), op=ALU.mult), R=[xt, rs], W=[xt])
                for c in range(8):
                    k.op('vector' if c % 2 else 'scalar',
                         (lambda e, c=c: e.tensor_scalar(out=ho[:, c, :N], in0=xt[:, c, :N], scalar1=gs[:, c, col:col + 1], scalar2=mod[:, c, col:col + 1], op0=ALU.mult, op1=ALU.add)) if c % 2 else
                         (lambda e, c=c: e.activation(out=ho[:, c, :N], in_=xt[:, c, :N], func=AF.Identity, scale=gs[:, c, col:col + 1], bias=mod[:, c, col:col + 1])),
                         R=[xt, gs, mod], W=[ho])
                for (a_, n_, h_, o_) in pieces(s0_, N):
                    k.dma(hT2v[h_][:, :, o_:o_ + n_], ho[:, :, a_:a_ + n_], R=[ho], W=[hD[ti]])
            k.barrier()

        hring = SH["hring"]

        def load_h(ti):
            t = hring.get()
            s0_, N = TT[ti]
            for (a_, n_, h_, o_) in pieces(s0_, N):
                k.dma(t[:, :, a_:a_ + n_], hT2v[h_][:, :, o_:o_ + n_], R=[hD[ti]], W=[t])
            return t

        def proj(h, N, wt, c0, M):
            p = ps()
            for kc in range(8):
                k.op('tensor', lambda e, kc=kc: e.matmul(p[:M, :N], lhsT=wt[:, kc, c0:c0 + M], rhs=h[:, kc, :N], start=(kc == 0), stop=(kc == 7)), R=[wt, h], W=[p], inc=(kc == 7))
            return p

        GD = SH["GD"]
        GTall_ = dint("GTall", [2 * 3072, NTB], BF16)

        def gather_rows(chunks):
            for c_ in chunks:
                k.collective("AllGather", PAIRS, GTl[c_ * 256:(c_ + 1) * 256, :], GTall_[c_ * 512:(c_ + 1) * 512, :], R=[GD], W=[SH["GALL"]])

        if "lru" in phases:
            with ExitStack() as s2:
                wl = k.sbuf([128, 8, 1024], BF16, "wl", s2)
                for kc in range(8):
                    k.dma(wl[:, kc, :], win[:, kc, XA:XA + 1024], W=[wl], q='gpsimd')
                wa = k.sbuf([128, 2, 4, 128], BF16, "wa", s2)
                wx = k.sbuf([128, 2, 4, 128], BF16, "wx", s2)
                k.dma(wa[:], wa_d, W=[wa], q='gpsimd')
                k.dma(wx[:], wx_d, W=[wx], q='gpsimd')
                cw = k.sbuf([128, 4, 4], F32, "cw", s2)
                cbv = k.sbuf([128, 4], F32, "cbv", s2)
                hba = k.sbuf([128, 8], F32, "hba", s2)
                hbx = k.sbuf([128, 8], F32, "hbx", s2)
                lam = k.sbuf([128, 8], F32, "lam", s2)
                k.dma(cw[:], cw_d, W=[cw]); k.dma(cbv[:], cb_d, W=[cbv])
                k.dma(hba[:], ba_d.rearrange("p a b -> p (a b)"), W=[hba])
                k.dma(hbx[:], bx_d.rearrange("p a b -> p (a b)"), W=[hbx])
                k.dma(lam[:], lam_d.rearrange("p a b -> p (a b)"), W=[lam])
                k.op('vector', lambda e: e.tensor_scalar(out=hba[:], in0=hba[:], scalar1=0.5, scalar2=None, op0=ALU.mult), R=[hba], W=[hba])
                k.op('vector', lambda e: e.tensor_scalar(out=hbx[:], in0=hbx[:], scalar1=0.5, scalar2=None, op0=ALU.mult), R=[hbx], W=[hbx])
                hcl = softplus_neg(k, s2, lam, 8, "spl")
                k.op('vector', lambda e: e.tensor_scalar(out=hcl[:], in0=hcl[:], scalar1=-4.0, scalar2=None, op0=ALU.mult), R=[hcl], W=[hcl])
                XAb = k.sbuf([128, S], F32, "XAb", s2)
                U = k.sbuf([128, S], F32, "U", s2)
                UB = k.sbuf([128, S], BF16, "UB", s2)
                SG = k.sbuf([128, S], BF16, "SG", s2)
                I = k.sbuf([128, S], F32, "I", s2)
                Sb = k.sbuf([128, S], F32, "Sb", s2)
                HF = k.sbuf([128, S], F32, "HF", s2)
                OB = k.sbuf([128, S], BF16, "OBl", s2)
                R2 = k.sbuf([128, S], F32, "R2", s2)
                I2 = k.sbuf([128, S], F32, "I2", s2)
                Sb2 = k.sbuf([128, S], F32, "Sb2", s2)
                for j in range(4):
                    for ti, (s0_, N) in enumerate(TT):
                        h = load_h(ti)
                        p = proj(h, N, wl, j * 128, 128)
                        k.op('scalar', lambda e: e.activation(out=XAb[:, s0_:s0_ + N], in_=p[:, :N], func=AF.Copy), R=[p], W=[XAb])
                        p2 = proj(h, N, wl, 512 + j * 128, 128)
                        k.op('scalar', lambda e: e.activation(out=SG[:, s0_:s0_ + N], in_=p2[:, :N], func=(AF.Copy if nogate == 2 else AF.Silu)), R=[p2], W=[SG])
                    k.op('vector', lambda e: e.tensor_scalar(out=U[:], in0=XAb[:], scalar1=cw[:, j, 2:3], scalar2=cbv[:, j:j + 1], op0=ALU.mult, op1=ALU.add), R=[XAb, cw, cbv], W=[U])
                    for (a0, Ls) in ((0, L), (L, S - L)):
                        for tap, (do, so, ln) in ((0, (2, 0, Ls - 2)), (1, (1, 0, Ls - 1)), (3, (0, 1, Ls - 1))):
                            k.op('vector', lambda e, tap=tap, do=do, so=so, ln=ln, a0=a0: e.scalar_tensor_tensor(
                                out=U[:, a0 + do:a0 + do + ln], in0=XAb[:, a0 + so:a0 + so + ln], scalar=cw[:, j, tap:tap + 1], op0=ALU.mult,
                                in1=U[:, a0 + do:a0 + do + ln], op1=ALU.add), R=[XAb, U, cw], W=[U])
                    k.op('scalar', lambda e: e.activation(out=UB[:], in_=U[:], func=AF.Copy), R=[U], W=[UB])
                    Rbs, Is, Sbs = [XAb, R2], [I, I2], [Sb, Sb2]
                    for ti, (s0_, N) in enumerate(TT):
                        for d in range(2):
                            dj = d * 4 + j
                            p = ps()
                            k.op('tensor', lambda e, d=d: e.matmul(p[:, :N], lhsT=wa[:, d, j, :], rhs=UB[:, s0_:s0_ + N], start=True, stop=True), R=[wa, UB], W=[p])
                            k.op('scalar', lambda e, d=d, dj=dj: e.activation(out=Rbs[d][:, s0_:s0_ + N], in_=p[:, :N], func=AF.Tanh, scale=0.5, bias=hba[:, dj:dj + 1]), R=[p, hba], W=[Rbs[d]])
                            p2 = ps()
                            k.op('tensor', lambda e, d=d: e.matmul(p2[:, :N], lhsT=wx[:, d, j, :], rhs=UB[:, s0_:s0_ + N], start=True, stop=True), R=[wx, UB], W=[p2])
                            k.op('scalar', lambda e, d=d, dj=dj: e.activation(out=Is[d][:, s0_:s0_ + N], in_=p2[:, :N], func=AF.Tanh, scale=0.5, bias=hbx[:, dj:dj + 1]), R=[p2, hbx], W=[Is[d]])
                    for d in range(2):
                        dj = d * 4 + j
                        k.op('scalar', lambda e, d=d, dj=dj: e.activation(out=Rbs[d][:], in_=Rbs[d][:], func=AF.Exp, scale=hcl[:, dj:dj + 1], bias=hcl[:, dj:dj + 1]), R=[Rbs[d], hcl], W=[Rbs[d]])
                    for d in range(2):
                        k.op('scalar', lambda e, d=d: e.activation(out=Sbs[d][:], in_=Rbs[d][:], func=AF.Square), R=[Rbs[d]], W=[Sbs[d]])
                    for d in range(2):
                        k.op('vector', lambda e, d=d: e.scalar_tensor_tensor(out=Is[d][:], in0=Is[d][:], scalar=1.0, op0=ALU.add, in1=U[:], op1=ALU.mult), R=[Is[d], U], W=[Is[d]])
                    for d in range(2):
                        k.op('scalar', lambda e, d=d: e.activation(out=Sbs[d][:], in_=Sbs[d][:], func=AF.Sqrt, scale=-0.25, bias=0.25), R=[Sbs[d]], W=[Sbs[d]])
                    for d in range(2):
                        k.op('vector', lambda e, d=d: e.tensor_tensor(out=Is[d][:], in0=Is[d][:], in1=Sbs[d][:], op=ALU.mult), R=[Is[d], Sbs[d]], W=[Is[d]])
                    k.op('vector', lambda e: e.tensor_tensor_scan(out=HF[:], data0=Rbs[0][:], data1=Is[0][:], initial=0.0, op0=ALU.mult, op1=ALU.add), R=[Rbs[0], Is[0]], W=[HF])
                    k.op('vector', lambda e: e.tensor_tensor_scan(out=rev_ap(Sb2[:, 0:L]), data0=rev_ap(R2[:, 0:L]), data1=rev_ap(I2[:, 0:L]), initial=0.0,
                                                                  op0=ALU.mult, op1=ALU.add), R=[R2, I2], W=[Sb2])
                    k.op('vector', lambda e: e.tensor_tensor_scan(out=rev_ap(Sb2[:, L:S]), data0=rev_ap(R2[:, L:S]), data1=rev_ap(I2[:, L:S]), initial=Sb2[:, 0:1],
                                                                  op0=ALU.mult, op1=ALU.add), R=[R2, I2, Sb2], W=[Sb2])
                    k.op('gpsimd', lambda e: e.tensor_tensor(out=HF[:], in0=HF[:], in1=Sb2[:], op=ALU.add), R=[HF, Sb2], W=[HF])
                    if nogate:
                        k.op('vector', lambda e: e.tensor_copy(out=OB[:], in_=(SG[:] if nogate == 2 else HF[:])), R=[HF, SG], W=[OB])
                    else:
                        k.op('vector', lambda e: e.tensor_tensor(out=OB[:], in0=HF[:], in1=SG[:], op=ALU.mult), R=[HF, SG], W=[OB])
                    for (a_, n_, h_, o_) in pieces(0, S):
                        k.dma(GT2[h_][j * 128:(j + 1) * 128, o_:o_ + n_], OB[:, a_:a_ + n_], R=[OB], W=[GD])
                k.barrier()
                gather_rows((0, 1, 6, 7))

        if "ret" in phases:
            with ExitStack() as s3:
                wr = k.sbuf([128, 8, 1024], BF16, "wr", s3)
                th = k.sbuf([128, 4], F32, "th", s3)
                k.dma(th[:], th_d.rearrange("p a b -> p (a b)"), W=[th])
                rg = k.sbuf([128, 2, 2], F32, "rg", s3)
                k.dma(rg[:], rg_d, W=[rg])
                cmask = k.sbuf([128, 4, 128], F32, "cmask", s3)
                k.dma(cmask[:], cmask_d, W=[cmask])
                cpos = k.sbuf([128, 2, 128], F32, "cpos", s3)
                k.dma(cpos[:], cpos_d, W=[cpos])
                kpos = k.sbuf([128, 2], F32, "kpos", s3)
                k.dma(kpos[:], kpos_d, W=[kpos])
                lg = softplus_neg(k, s3, th, 4, "spt")
                k.op('vector', lambda e: e.tensor_scalar(out=lg[:], in0=lg[:], scalar1=-1.0, scalar2=None, op0=ALU.mult), R=[lg], W=[lg])
                masks = k.sbuf([128, 4, 128], F32, "masks", s3)
                qdec = k.sbuf([128, 4, 128], F32, "qdec", s3)
                kdec = k.sbuf([128, 4], F32, "kdec", s3)
                gC = k.sbuf([128, 4], F32, "gC", s3)
                for d in range(2):
                    for hd in range(2):
                        idx = d * 2 + hd
                        k.op('scalar', lambda e, d=d, idx=idx: e.activation(out=masks[:, idx, :], in_=cmask[:, 2 * d, :], func=AF.Exp, scale=lg[:, idx:idx + 1]), R=[cmask, lg], W=[masks])
                        k.op('vector', lambda e, d=d, idx=idx: e.tensor_tensor(out=masks[:, idx, :], in0=masks[:, idx, :], in1=cmask[:, 2 * d + 1, :], op=ALU.mult), R=[masks, cmask], W=[masks])
                        k.op('scalar', lambda e, d=d, idx=idx: e.activation(out=qdec[:, idx, :], in_=cpos[:, d, :], func=AF.Exp, scale=lg[:, idx:idx + 1]), R=[cpos, lg], W=[qdec])
                        k.op('scalar', lambda e, d=d, idx=idx: e.activation(out=kdec[:, idx:idx + 1], in_=kpos[:, d:d + 1], func=AF.Exp, scale=lg[:, idx:idx + 1]), R=[kpos, lg], W=[kdec])
                        k.op('scalar', lambda e, idx=idx: e.activation(out=gC[:, idx:idx + 1], in_=lg[:, idx:idx + 1], func=AF.Exp, scale=128.0), R=[lg], W=[gC])
                QT = k.sbuf([128, 2, S], BF16, "QT", s3)
                KT_ = k.sbuf([128, 2, S], BF16, "KT", s3)
                V = k.sbuf([128, NCH, 256], BF16, "Vr", s3)
                OF = k.sbuf([128, 2, S], F32, "OF", s3)
                OBk = k.sbuf([128, 2, S], F32, "OBk", s3)
                order = [list(range(NCH)), [1, 0] + list(range(NCH - 1, 1, -1))]
                for hd in range(2):
                  for i_, base_ in enumerate((QQ, KK, VV, GB)):
                      k.dma(wr[:, :, i_ * 256:(i_ + 1) * 256], win[:, :, base_ + hd * 256:base_ + hd * 256 + 256], W=[wr], q='gpsimd')
                  with ExitStack() as sA:
                    rtab = k.sbuf([128, 2, 2, 512], F32, "rtab", sA)
                    rawr = Ring([k.sbuf([128, 512], BF16, f"raw{i}", sA) for i in range(2)])
                    t1r = Ring([k.sbuf([128, 512], F32, f"t1r{i}", sA) for i in range(2)])
                    t2r = Ring([k.sbuf([128, 512], F32, f"t2r{i}", sA) for i in range(2)])
                    ksr = Ring([k.sbuf([128, 256], BF16, f"ks{i}", sA) for i in range(6)])
                    attr = Ring([k.sbuf([128, 128], BF16, f"att{i}", sA) for i in range(6)])
                    tmpr = Ring([k.sbuf([128, 2, 128], F32, f"tmp{i}", sA) for i in range(4)])
                    S32 = [k.sbuf([128, 2, 256], F32, f"S32_{d}", sA) for d in range(2)]
                    Sbf = [k.sbuf([128, 2, 256], BF16, f"Sbf_{d}", sA) for d in range(2)]
                    import os as _os
                    for ti, (s0_, N) in enumerate(TT if int(_os.environ.get('RET_STOP', '9')) > 0 else []):
                        h = load_h(ti)
                        if ti > 0:
                            k.dma(rtab[:], ropeRT_d[:, :, :, (ti - 1) * 512:ti * 512], W=[rtab])
                        for dst, cb_ in ((QT, 0), (KT_, 256)):
                            for dc in range(2):
                                p = proj(h, N, wr, cb_ + dc * 128, 128)
                                if ti == 0 or _os.environ.get('RET_NOROPE'):
                                    k.op('scalar', lambda e: e.activation(out=dst[:, dc, s0_:s0_ + N], in_=p[:, :N], func=AF.Copy), R=[p], W=[dst])
                                    continue
                                raw = rawr.get(); t1 = t1r.get(); t2 = t2r.get()
                                RM = int(_os.environ.get('ROPE_MODE', '5'))
                                k.op('scalar', lambda e: e.activation(out=raw[:], in_=p[:, :], func=AF.Copy), R=[p], W=[raw])
                                if RM >= 2:
                                    pr = ps()
                                    k.op('tensor', lambda e: e.matmul(pr[:, :], lhsT=permR[:], rhs=raw[:], start=True, stop=True), R=[permR, raw], W=[pr])
                                if RM >= 3:
                                    k.op('vector', lambda e: e.tensor_tensor(out=t1[:], in0=p[:, :], in1=rtab[:, 0, dc, :], op=ALU.mult), R=[p, rtab], W=[t1])
                                if RM >= 4:
                                    k.op('vector', lambda e: e.tensor_tensor(out=t2[:], in0=pr[:, :], in1=rtab[:, 1, dc, :], op=ALU.mult), R=[pr, rtab], W=[t2])
                                if RM >= 5:
                                    k.op('vector', lambda e: e.tensor_tensor(out=dst[:, dc, s0_:s0_ + N], in0=t1[:], in1=t2[:], op=ALU.add), R=[t1, t2], W=[dst])
                                else:
                                    k.op('scalar', lambda e: e.activation(out=dst[:, dc, s0_:s0_ + N], in_=p[:, :N], func=AF.Copy), R=[p], W=[dst])
                        for jj in range(0 if _os.environ.get('RET_NOV') else N // 128):
                            pv = ps()
                            for kc in range(8):
                                k.op('tensor', lambda e, kc=kc: e.matmul(pv[:, :256], lhsT=h[:, kc, jj * 128:(jj + 1) * 128], rhs=wr[:, kc, 512:768],
                                                                      start=(kc == 0), stop=(kc == 7)), R=[h, wr], W=[pv], inc=(kc == 7))
                            ch = (s0_ + jj * 128) // 128
                            k.op('scalar', lambda e: e.activation(out=V[:, ch, :], in_=pv[:, :256], func=AF.Copy), R=[pv], W=[V])
                    import os as _os
                    rot4 = Ring(banks[:4])
                    Dbank = [[banks[4], banks[5]], [banks[6], banks[7]]]

                    def stage1(step, d):
                        c = order[d][step]; idx = d * 2 + hd; c0 = c * 128
                        dst = (OF, OBk)[d]
                        pt = rot4.get()
                        ptb = pt[:, :].bitcast(BF16)
                        for dc in range(2):
                            k.op('tensor', lambda e, dc=dc: e.transpose(out=ptb[:, dc * 128:(dc + 1) * 128], in_=KT_[:, dc, c0:c0 + 128], identity=ident[:]), R=[KT_, ident], W=[pt])
                        ks = ksr.get()
                        k.op('vector', lambda e: e.tensor_scalar(out=ks[:], in0=ptb[:, 0:256], scalar1=kdec[:, idx:idx + 1], scalar2=None, op0=ALU.mult), R=[pt, kdec], W=[ks])
                        sT = rot4.get()
                        for dc in range(2):
                            k.op('tensor', lambda e, dc=dc: e.matmul(sT[:, :128], lhsT=KT_[:, dc, c0:c0 + 128], rhs=QT[:, dc, c0:c0 + 128], start=(dc == 0), stop=(dc == 1)), R=[KT_, QT], W=[sT], inc=(dc == 1))
                        att = attr.get()
                        k.op('vector', lambda e: e.tensor_tensor(out=att[:], in0=sT[:, :128], in1=masks[:, idx, :], op=ALU.mult), R=[sT, masks], W=[att])
                        A = rot4.get()
                        for ec in range(2):
                            k.op('tensor', lambda e, ec=ec: e.matmul(A[:, ec * 128:(ec + 1) * 128], lhsT=V[:, c, ec * 128:(ec + 1) * 128], rhs=att[:], start=True, stop=True), R=[V, att], W=[A])
                        k.op('scalar', lambda e: e.activation(out=dst[:, :, c0:c0 + 128], in_=A[:, 0:256].rearrange("p (a n) -> p a n", a=2), func=AF.Copy), R=[A], W=[dst])
                        if step < NCH - 1:
                            D = Dbank[d][step % 2]
                            for dc in range(2):
                                k.op('tensor', lambda e, dc=dc: e.matmul(D[:, dc * 256:(dc + 1) * 256], lhsT=ks[:, dc * 128:(dc + 1) * 128], rhs=V[:, c, :], start=True, stop=True), R=[ks, V], W=[D])

                    def stage2(step, d):
                        c = order[d][step]; idx = d * 2 + hd; c0 = c * 128
                        dst = (OF, OBk)[d]
                        if step > 0:
                            B = rot4.get()
                            for ec in range(2):
                                for dc in range(2):
                                    k.op('tensor', lambda e, ec=ec, dc=dc: e.matmul(B[:, ec * 128:(ec + 1) * 128], lhsT=Sbf[d][:, dc, ec * 128:(ec + 1) * 128], rhs=QT[:, dc, c0:c0 + 128],
                                                                                  start=(dc == 0), stop=(dc == 1)), R=[Sbf[d], QT], W=[B])
                            tmp = tmpr.get()
                            k.op('vector', lambda e: e.tensor_tensor(out=tmp[:], in0=B[:, 0:256].rearrange("p (a n) -> p a n", a=2),
                                                                     in1=qdec[:, idx, :].unsqueeze(1).to_broadcast([128, 2, 128]), op=ALU.mult), R=[B, qdec], W=[tmp])
                            k.op('gpsimd', lambda e: e.tensor_tensor(out=dst[:, :, c0:c0 + 128], in0=dst[:, :, c0:c0 + 128], in1=tmp[:], op=ALU.add), R=[dst, tmp], W=[dst])
                        if step < NCH - 1:
                            D = Dbank[d][step % 2]
                            Dv = D[:, :].rearrange("p (a n) -> p a n", a=2)
                            if step == 0:
                                k.op('scalar', lambda e: e.activation(out=S32[d][:], in_=Dv, func=AF.Copy), R=[D], W=[S32[d]])
                            else:
                                k.op('vector', lambda e: e.scalar_tensor_tensor(out=S32[d][:], in0=S32[d][:], scalar=gC[:, idx:idx + 1], op0=ALU.mult, in1=Dv, op1=ALU.add), R=[S32[d], D, gC], W=[S32[d]])
                            k.op('scalar', lambda e: e.activation(out=Sbf[d][:], in_=S32[d][:], func=AF.Copy), R=[S32[d]], W=[Sbf[d]])

                    for d in range(2):
                        stage1(0, d)
                    for step in range(NCH):
                        if step + 1 < NCH:
                            for d in range(2):
                                stage1(step + 1, d)
                        for d in range(2):
                            stage2(step, d)
                    k.barrier()
                  with ExitStack() as sB:
                    o_r = Ring([k.sbuf([128, 2, 512], F32, f"o_r{i}", sB) for i in range(1)])
                    ob_r = Ring([k.sbuf([128, 2, 512], BF16, f"ob_r{i}", sB) for i in range(1)])
                    sq_r = Ring([k.sbuf([128, 2, 512], BF16, f"sq_r{i}", sB) for i in range(1)])
                    st_r = Ring([k.sbuf([128, 4, 512], F32, f"st_r{i}", sB) for i in range(1)])
                    sg_r = Ring([k.sbuf([128, 512], BF16, f"sg_r{i}", sB) for i in range(2)])
                    y_r = Ring([k.sbuf([128, 512], F32, f"y_r{i}", sB) for i in range(2)])
                    oo_r = Ring([k.sbuf([128, 512], BF16, f"oo_r{i}", sB) for i in range(2)])
                    for ti, (s0_, N) in enumerate(TT if int(_os.environ.get('RET_STOP', '9')) > 2 else []):
                        h = load_h(ti)
                        o = o_r.get(); ob = ob_r.get(); sq = sq_r.get(); stt_ = st_r.get()
                        k.op('vector', lambda e: e.tensor_tensor(out=o[:, :, :N], in0=OF[:, :, s0_:s0_ + N], in1=OBk[:, :, s0_:s0_ + N], op=ALU.add), R=[OF, OBk], W=[o])
                        k.op('scalar', lambda e: e.activation(out=ob[:, :, :N], in_=o[:, :, :N], func=AF.Copy), R=[o], W=[ob])
                        k.op('scalar', lambda e: e.activation(out=sq[:, :, :N], in_=o[:, :, :N], func=AF.Square), R=[o], W=[sq])
                        p1 = ps(); p2 = ps()
                        for ec in range(2):
                            k.op('tensor', lambda e, ec=ec: e.matmul(p1[:, :N], lhsT=ones[:], rhs=ob[:, ec, :N], start=(ec == 0), stop=(ec == 1)), R=[ones, ob], W=[p1], inc=(ec == 1))
                        for ec in range(2):
                            k.op('tensor', lambda e, ec=ec: e.matmul(p2[:, :N], lhsT=ones[:], rhs=sq[:, ec, :N], start=(ec == 0), stop=(ec == 1)), R=[ones, sq], W=[p2], inc=(ec == 1))
                        mean, var, rstd, nmr = (stt_[:, i, :N] for i in range(4))
                        k.op('scalar', lambda e: e.activation(out=mean, in_=p1[:, :N], func=AF.Copy, scale=1.0 / 256.0), R=[p1], W=[stt_])
                        k.op('vector', lambda e: e.tensor_tensor(out=var, in0=mean, in1=mean, op=ALU.mult), R=[stt_], W=[stt_])
                        k.op('vector', lambda e: e.scalar_tensor_tensor(out=var, in0=p2[:, :N], scalar=1.0 / 256.0, op0=ALU.mult, in1=var, op1=ALU.subtract), R=[p2, stt_], W=[stt_])
                        k.op('scalar', lambda e: e.activation(out=rstd, in_=var, func=AF.Ln, bias=256.0 * EPS), R=[stt_], W=[stt_])
                        k.op('scalar', lambda e: e.activation(out=rstd, in_=rstd, func=AF.Exp, scale=-0.5), R=[stt_], W=[stt_])
                        k.op('vector', lambda e: e.scalar_tensor_tensor(out=nmr, in0=mean, scalar=-1.0, op0=ALU.mult, in1=rstd, op1=ALU.mult), R=[stt_], W=[stt_])
                        for ec in range(2):
                            pg = proj(h, N, wr, 768 + ec * 128, 128)
                            sg = sg_r.get(); y = y_r.get(); oo = oo_r.get()
                            k.op('scalar', lambda e: e.activation(out=sg[:, :N], in_=pg[:, :N], func=AF.Silu), R=[pg], W=[sg])
                            k.op('vector', lambda e: e.tensor_tensor(out=y[:, :N], in0=o[:, ec, :N], in1=rstd, op=ALU.mult), R=[o, stt_], W=[y])
                            k.op('vector', lambda e: e.tensor_tensor(out=y[:, :N], in0=y[:, :N], in1=nmr, op=ALU.add), R=[y, stt_], W=[y])
                            if nogate:
                                k.op('vector', lambda e: e.tensor_scalar(out=oo[:, :N], in0=y[:, :N], scalar1=rg[:, hd, ec:ec + 1], scalar2=None, op0=ALU.mult), R=[y, rg], W=[oo])
                            else:
                                k.op('vector', lambda e: e.scalar_tensor_tensor(out=oo[:, :N], in0=y[:, :N], scalar=rg[:, hd, ec:ec + 1], op0=ALU.mult, in1=sg[:, :N], op1=ALU.mult), R=[y, rg, sg], W=[oo])
                            r0_ = 512 + hd * 256 + ec * 128
                            for (a_, n_, h_, o_) in pieces(s0_, N):
                                k.dma(GT2[h_][r0_:r0_ + 128, o_:o_ + n_], oo[:, a_:a_ + n_], R=[oo], W=[GD])
                    k.barrier()
                    if hd == 1:
                        gather_rows((2, 3, 8, 9))

        if "mla" in phases:
            MSCALE = 192.0 ** -0.5
            with ExitStack() as s4:
                wqr = k.sbuf([128, 3, 4, 128], BF16, "wqr", s4)
                k.op('vector', lambda e: e.memset(wqr[:], 0.0), W=[wqr])
                for hh_ in range(4):
                    k.dma(wqr[:, :, hh_, 0:64], wqup_d[:, :, hh_ * 192 + 128:hh_ * 192 + 192], W=[wqr], q='gpsimd')
                wq = k.sbuf([128, 3, 768], BF16, "wq", s4)
                k.dma(wq[:], wqup_d, W=[wq], q='gpsimd')
                wkv = k.sbuf([128, 2, 1024], BF16, "wkv", s4)
                k.dma(wkv[:], wkvup_d, W=[wkv], q='gpsimd')
                wgc = k.sbuf([128, 8, 128], BF16, "wgc", s4)
                qn = k.sbuf([128, 3], F32, "qn", s4); k.dma(qn[:], qn_d, W=[qn])
                kvn = k.sbuf([128, 2], F32, "kvn", s4); k.dma(kvn[:], kvn_d, W=[kvn])
                g4 = k.sbuf([128, 4], F32, "g4", s4); k.dma(g4[:], gq_d, W=[g4])
                QDN = k.sbuf([128, 3, S], BF16, "QDN", s4)
                KVN = k.sbuf([128, 2, S], BF16, "KVN", s4)
                KROT = k.sbuf([128, S], BF16, "KROT", s4)
                rsr_ = Ring([k.sbuf([128, 512], F32, f"mrs{i}", s4) for i in range(2)])
                f32r = Ring([k.sbuf([128, 512], F32, f"mf{i}", s4) for i in range(4)])
                rawr_ = Ring([k.sbuf([128, 512], BF16, f"mraw{i}", s4) for i in range(2)])
                mtab = k.sbuf([128, 2, 512], F32, "mtab", s4)

                def proj_(h, N, wt, c0, M, psf):
                    p = psf()
                    for kc in range(8):
                        k.op('tensor', lambda e, kc=kc: e.matmul(p[:M, :N], lhsT=wt[:, kc, c0:c0 + M], rhs=h[:, kc, :N], start=(kc == 0), stop=(kc == 7)), R=[wt, h], W=[p], inc=(kc == 7))
                    return p

                def rmsn(pts, P, N, dim, gains, outs, out_tile, sqring, psf):
                    sq = sqring.get(); rs = rsr_.get()
                    for c, pt in enumerate(pts):
                        k.op('scalar', lambda e, c=c, pt=pt: e.activation(out=sq[:P, c, :N], in_=pt[:P, :N], func=AF.Square), R=[pt], W=[sq])
                    yield
                    pss = psf()
                    for c in range(len(pts)):
                        k.op('tensor', lambda e, c=c: e.matmul(pss[:P, :N], lhsT=ones[:P, :P], rhs=sq[:P, c, :N], start=(c == 0), stop=(c == len(pts) - 1)), R=[ones, sq], W=[pss], inc=(c == len(pts) - 1))
                    yield
                    k.op('scalar', lambda e: e.activation(out=rs[:P, :N], in_=pss[:P, :N], func=AF.Ln, scale=1.0 / dim, bias=EPS), R=[pss], W=[rs])
                    k.op('scalar', lambda e: e.activation(out=rs[:P, :N], in_=rs[:P, :N], func=AF.Exp, scale=-0.5), R=[rs], W=[rs])
                    yield
                    for c, pt in enumerate(pts):
                        k.op('vector', lambda e, c=c, pt=pt: e.scalar_tensor_tensor(out=outs[c], in0=pt[:P, :N], scalar=gains[c], op0=ALU.mult, in1=rs[:P, :N], op1=ALU.mult), R=[pt, rs] + out_tile[1:], W=[out_tile[0]])

                def rope64(src, N, out_ap, out_tile, psf):
                    raw = rawr_.get(); t1 = f32r.get(); t2 = f32r.get()
                    yield
                    k.op('scalar', lambda e: e.activation(out=raw[:, :N], in_=src[:, :N], func=AF.Copy), R=[src], W=[raw])
                    k.op('vector', lambda e: e.tensor_tensor(out=t1[:, :N], in0=src[:, :N], in1=mtab[:, 0, :N], op=ALU.mult), R=[src, mtab], W=[t1])
                    yield
                    pr = psf()
                    k.op('tensor', lambda e: e.matmul(pr[:, :N], lhsT=permM[:], rhs=raw[:, :N], start=True, stop=True), R=[permM, raw], W=[pr])
                    yield
                    k.op('vector', lambda e: e.tensor_tensor(out=t2[:, :N], in0=pr[:, :N], in1=mtab[:, 1, :N], op=ALU.mult), R=[pr, mtab], W=[t2])
                    k.op('vector', lambda e: e.tensor_tensor(out=out_ap, in0=t1[:, :N], in1=t2[:, :N], op=ALU.add), R=[t1, t2], W=[out_tile])

                def run(gen):
                    for _ in gen:
                        pass

                with ExitStack() as sC1:
                    wm_ = k.sbuf([128, 8, 768], BF16, "wmla", sC1)
                    k.op('vector', lambda e: e.memset(wm_[:], 0.0), W=[wm_])
                    k.dma(wm_[:, :, 0:704], win[:, :, QD:QD + 704], W=[wm_], q='gpsimd')
                    sq3 = Ring([k.sbuf([128, 3, 512], BF16, f"msq{i}", sC1) for i in range(2)])
                    for ti, (s0_, N) in enumerate(TT):
                        h = load_h(ti)
                        if ti > 0:
                            k.dma(mtab[:], ropeMT_d[:, :, (ti - 1) * 512:ti * 512], W=[mtab])
                        pcs = [proj_(h, N, wm_, c * 128, 128, ps) for c in range(3)]
                        run(rmsn(pcs, 128, N, 384.0, [qn[:, c:c + 1] for c in range(3)], [QDN[:, c, s0_:s0_ + N] for c in range(3)], [QDN, qn], sq3, ps))
                        pcs = [proj_(h, N, wm_, 384 + c * 128, 128, ps) for c in range(2)]
                        run(rmsn(pcs, 128, N, 256.0, [kvn[:, c:c + 1] for c in range(2)], [KVN[:, c, s0_:s0_ + N] for c in range(2)], [KVN, kvn], sq3, ps))
                        pk = proj_(h, N, wm_, 640, 128, ps)
                        if ti == 0:
                            run(rmsn([pk], 128, N, 64.0, [g4[:, 3:4]], [KROT[:, s0_:s0_ + N]], [KROT, g4], sq3, ps))
                        else:
                            kf = f32r.get()
                            run(rmsn([pk], 128, N, 64.0, [g4[:, 3:4]], [kf[:, :N]], [kf, g4], sq3, ps))
                            run(rope64(kf, N, KROT[:, s0_:s0_ + N], KROT, ps))
                    k.barrier()
                HB = [dict(KTh=k.sbuf([128, S], BF16, f"KTh{i}", s4), Vh=k.sbuf([128, NCH, 128], BF16, f"Vh{i}", s4),
                           QNh=k.sbuf([128, S], BF16, f"QNh{i}", s4), QRh=k.sbuf([128, S], BF16, f"QRh{i}", s4),
                           SGC=k.sbuf([128, S], BF16, f"SGC{i}", s4)) for i in range(2)]
                sq1 = Ring([k.sbuf([128, 1, 512], BF16, f"msq1_{i}", s4) for i in range(2)])
                pTr = Ring([k.sbuf([128, 512], BF16, f"pT{i}", s4) for i in range(5)])
                oor = Ring([k.sbuf([128, 512], BF16, f"moo{i}", s4) for i in range(2)])
                rl_a = k.sbuf([128, 512], F32, "rl_a", s4)
                y_a = k.sbuf([128, 512], F32, "y_a", s4)
                psA = Ring(banks[0:4]).get
                psP = Ring(banks[4:6]).get

                def prep(hh):
                    B_ = HB[hh % 2]
                    KTh, Vh, QNh, QRh, SGC = B_["KTh"], B_["Vh"], B_["QNh"], B_["QRh"], B_["SGC"]
                    k.dma(wgc[:], win[:, :, GC + hh * 128:GC + hh * 128 + 128], W=[wgc], q='gpsimd')
                    for ti, (s0_, N) in enumerate(TT):
                        h = load_h(ti)
                        if ti > 0:
                            k.dma(mtab[:], ropeMT_d[:, :, (ti - 1) * 512:ti * 512], W=[mtab])
                        pg = proj_(h, N, wgc, 0, 128, psP)
                        yield
                        k.op('scalar', lambda e: e.activation(out=SGC[:, s0_:s0_ + N], in_=pg[:, :N], func=AF.Silu), R=[pg], W=[SGC])
                        pkn = psP()
                        for c in range(2):
                            k.op('tensor', lambda e, c=c: e.matmul(pkn[:, :N], lhsT=wkv[:, c, hh * 256:hh * 256 + 128], rhs=KVN[:, c, s0_:s0_ + N], start=(c == 0), stop=(c == 1)), R=[wkv, KVN], W=[pkn], inc=(c == 1))
                        yield
                        yield from rmsn([pkn], 128, N, 128.0, [g4[:, 2:3]], [KTh[:, s0_:s0_ + N]], [KTh, g4], sq1, psP)
                        for jj in range(N // 128):
                            pv = psP()
                            for c in range(2):
                                k.op('tensor', lambda e, c=c: e.matmul(pv[:, :128], lhsT=KVN[:, c, s0_ + jj * 128:s0_ + (jj + 1) * 128], rhs=wkv[:, c, hh * 256 + 128:hh * 256 + 256],
                                                                     start=(c == 0), stop=(c == 1)), R=[KVN, wkv], W=[pv], inc=(c == 1))
                            yield
                            ch = (s0_ + jj * 128) // 128
                            k.op('scalar', lambda e: e.activation(out=Vh[:, ch, :], in_=pv[:, :128], func=AF.Copy), R=[pv], W=[Vh])
                        pq = psP()
                        for c in range(3):
                            k.op('tensor', lambda e, c=c: e.matmul(pq[:, :N], lhsT=wq[:, c, hh * 192:hh * 192 + 128], rhs=QDN[:, c, s0_:s0_ + N], start=(c == 0), stop=(c == 2)), R=[wq, QDN], W=[pq], inc=(c == 2))
                        yield
                        yield from rmsn([pq], 128, N, 128.0, [g4[:, 0:1]], [QNh[:, s0_:s0_ + N]], [QNh, g4], sq1, psP)
                        pqr = psP()
                        for c in range(3):
                            k.op('tensor', lambda e, c=c: e.matmul(pqr[:, :N], lhsT=wqr[:, c, hh, :], rhs=QDN[:, c, s0_:s0_ + N], start=(c == 0), stop=(c == 2)), R=[wqr, QDN], W=[pqr], inc=(c == 2))
                        yield
                        if ti == 0:
                            yield from rmsn([pqr], 128, N, 64.0, [g4[:, 1:2]], [QRh[:, s0_:s0_ + N]], [QRh, g4], sq1, psP)
                        else:
                            qf = f32r.get()
                            yield from rmsn([pqr], 128, N, 64.0, [g4[:, 1:2]], [qf[:, :N]], [qf, g4], sq1, psP)
                            yield from rope64(qf, N, QRh[:, s0_:s0_ + N], QRh, psP)
                        yield

                def attn(hh):
                    B_ = HB[hh % 2]
                    KTh, Vh, QNh, QRh, SGC = B_["KTh"], B_["Vh"], B_["QNh"], B_["QRh"], B_["SGC"]
                    for qi, (q0, N) in enumerate(TT):
                        kts = list(range(2)) if qi == 0 else list(range(NCH))

                        def score(kt):
                            sT = psA()
                            k.op('tensor', lambda e: e.matmul(sT[:, :N], lhsT=KTh[:, kt * 128:(kt + 1) * 128], rhs=QNh[:, q0:q0 + N], start=True, stop=False), R=[KTh, QNh], W=[sT], inc=False)
                            k.op('tensor', lambda e: e.matmul(sT[:, :N], lhsT=KROT[:, kt * 128:(kt + 1) * 128], rhs=QRh[:, q0:q0 + N], start=False, stop=True), R=[KROT, QRh], W=[sT])
                            return sT
                        LOOK = 3
                        pend = [score(kt) for kt in kts[:LOOK]]
                        for i_, kt in enumerate(kts):
                            if i_ + LOOK < len(kts):
                                pend.append(score(kts[i_ + LOOK]))
                            sT = pend.pop(0)
                            pT = pTr.get()
                            k.op('scalar', lambda e: e.activation(out=pT[:, :N], in_=sT[:, :N], func=AF.Exp, scale=MSCALE), R=[sT], W=[pT])
                            k.op('tensor', lambda e: e.matmul(accO[:, :N], lhsT=Vh[:, kt, :], rhs=pT[:, :N], start=(kt == kts[0]), stop=(kt == kts[-1])), R=[Vh, pT], W=[accO], inc=(kt == kts[-1]))
                            k.op('tensor', lambda e: e.matmul(accL[:, :N], lhsT=ones[:], rhs=pT[:, :N], start=(kt == kts[0]), stop=(kt == kts[-1])), R=[ones, pT], W=[accL], inc=(kt == kts[-1]))
                            yield
                        rl = rl_a; y = y_a; oo = oor.get()
                        k.op('scalar', lambda e: e.activation(out=rl[:, :N], in_=accL[:, :N], func=AF.Ln), R=[accL], W=[rl])
                        k.op('scalar', lambda e: e.activation(out=rl[:, :N], in_=rl[:, :N], func=AF.Exp, scale=-1.0), R=[rl], W=[rl])
                        k.op('vector', lambda e: e.tensor_tensor(out=y[:, :N], in0=accO[:, :N], in1=rl[:, :N], op=ALU.mult), R=[accO, rl], W=[y])
                        if nogate:
                            k.op('vector', lambda e: e.tensor_copy(out=oo[:, :N], in_=y[:, :N]), R=[y], W=[oo])
                        else:
                            k.op('vector', lambda e: e.tensor_tensor(out=oo[:, :N], in0=y[:, :N], in1=SGC[:, q0:q0 + N], op=ALU.mult), R=[y, SGC], W=[oo])
                        r0_ = 1024 + hh * 128
                        for (a_, n_, h_, o_) in pieces(q0, N):
                            k.dma(GT2[h_][r0_:r0_ + 128, o_:o_ + n_], oo[:, a_:a_ + n_], R=[oo], W=[GD])
                        yield

                run(prep(0))
                for hh in range(4):
                    nxt = prep(hh + 1) if hh < 3 else None
                    for _ in attn(hh):
                        if nxt is not None:
                            try:
                                next(nxt)
                            except StopIteration:
                                nxt = None
                    if nxt is not None:
                        run(nxt)
                k.barrier()
                gather_rows((4, 5, 10, 11))

    return mod, GTl


def _consts():
    c = {}
    c["c_ident"] = np.eye(128, dtype=np.float32)
    pr = np.zeros((128, 128), np.float32)
    for j in range(64):
        pr[j + 64, j] = -1.0
        pr[j, j + 64] = 1.0
    c["c_permR"] = pr
    pm = np.zeros((128, 128), np.float32)
    for j in range(16):
        pm[j + 16, j] = -1.0
        pm[j, j + 16] = 1.0
        pm[j + 48, j + 32] = -1.0
        pm[j + 32, j + 48] = 1.0
    c["c_permM"] = pm
    pos = np.arange(64, dtype=np.float32)
    invR = (np.float32(10000.0) ** (-np.arange(64, dtype=np.float32) / np.float32(64))).astype(np.float32)
    angR = (pos[None, :] * invR[np.arange(128) % 64][:, None]).astype(np.float32)
    c["c_ropeR"] = np.stack([np.cos(angR), np.sin(angR)], axis=1).astype(np.float32)
    invM = (np.float32(10000.0) ** (-np.arange(16, dtype=np.float32) / np.float32(16))).astype(np.float32)
    angM = (pos[None, :] * invM[np.arange(64) % 16][:, None]).astype(np.float32)
    c["c_ropeM"] = np.stack([np.cos(angM), np.sin(angM)], axis=1).astype(np.float32)
    tt = np.arange(4096)
    rowi, coli = tt // 64, tt % 64
    rt = np.zeros((128, 2, 2, 4096), np.float32)
    rt[:, :, 0, :] = c["c_ropeR"][:, :, rowi]
    rt[:, :, 1, :] = c["c_ropeR"][:, :, coli]
    c["c_ropeRT"] = rt
    mt = np.zeros((128, 2, 4096), np.float32)
    mt[:32] = c["c_ropeM"][:32][:, :, rowi]
    mt[32:64] = c["c_ropeM"][32:][:, :, coli]
    c["c_ropeMT"] = mt
    m = np.arange(128)[:, None]
    n = np.arange(128)[None, :]
    mk = np.zeros((128, 4, 128), np.float32)
    mk[:, 0] = np.maximum(n - m, 0)
    mk[:, 1] = (n >= m)
    mk[:, 2] = np.maximum(m - n, 0)
    mk[:, 3] = (m >= n)
    c["c_mask"] = mk
    cp = np.zeros((128, 2, 128), np.float32)
    cp[:, 0, :] = np.arange(128) + 1
    cp[:, 1, :] = 128 - np.arange(128)
    c["c_pos"] = cp
    kp = np.zeros((128, 2), np.float32)
    kp[:, 0] = 127 - np.arange(128)
    kp[:, 1] = np.arange(128)
    c["c_kpos"] = kp
    return c


def _pc(v, nchunk):
    return np.ascontiguousarray(np.asarray(v).reshape(nchunk, 128).T)


def a_inputs(inp, l, xT_all, consts):
    maps = []
    w_in = inp["w_in"][l]
    for core in range(8):
        b, p = core // 2, core % 2
        m = dict(consts)
        m["xT"] = np.ascontiguousarray(xT_all[b])
        cc = np.stack([inp["c"][b], inp["c_ctx"]], axis=-1)
        m["cT"] = np.ascontiguousarray(cc.reshape(8, 128, 2).transpose(1, 0, 2))
        m["wmod"] = inp["w_mod"][l]
        m["bmod"] = _pc(inp["b_mod"][l], 24)
        m["ngain"] = _pc(inp["norm_gain"][l], 8)
        cols = np.concatenate([
            np.arange(0 + p * 512, 0 + p * 512 + 512), np.arange(1024 + p * 512, 1024 + p * 512 + 512),
            np.arange(2048 + p * 512, 2048 + p * 512 + 512), np.arange(3072 + p * 512, 3072 + p * 512 + 512),
            np.arange(4096 + p * 512, 4096 + p * 512 + 512), np.arange(5120 + p * 512, 5120 + p * 512 + 512),
            np.arange(6144, 6848), np.arange(6848 + p * 512, 6848 + p * 512 + 512)])
        m["win"] = np.ascontiguousarray(w_in[:, cols])
        sl = slice(p * 512, (p + 1) * 512)
        m["convw"] = np.ascontiguousarray(inp["conv_w"][l][:, sl].reshape(4, 4, 128).transpose(2, 1, 0))
        m["convb"] = _pc(inp["conv_b"][l][sl], 4)
        m["lru_wa"] = np.ascontiguousarray(inp["lru_wa"][l][:, 4 * p:4 * p + 4].transpose(2, 0, 1, 3))
        m["lru_wx"] = np.ascontiguousarray(inp["lru_wx"][l][:, 4 * p:4 * p + 4].transpose(2, 0, 1, 3))
        for nm in ("lru_ba", "lru_bx"):
            m[nm] = np.ascontiguousarray(inp[nm][l][:, sl].reshape(2, 4, 128).transpose(2, 0, 1))
        m["lru_lam"] = np.ascontiguousarray(inp["lru_lambda"][l][:, sl].reshape(2, 4, 128).transpose(2, 0, 1))
        m["ret_theta"] = np.ascontiguousarray(np.broadcast_to(inp["ret_theta"][l][:, 2 * p:2 * p + 2][None], (128, 2, 2)))
        m["ret_gain"] = np.ascontiguousarray(inp["ret_gain"][l][sl].reshape(2, 2, 128).transpose(2, 0, 1))
        m["mla_qnorm"] = _pc(inp["mla_q_norm"][l], 3)
        m["mla_wqup"] = np.ascontiguousarray(inp["mla_w_q_up"][l][:, p * 768:(p + 1) * 768])
        m["mla_kvnorm"] = _pc(inp["mla_kv_norm"][l], 2)
        m["mla_wkvup"] = np.ascontiguousarray(inp["mla_w_kv_up"][l][:, p * 1024:(p + 1) * 1024])
        g4 = np.zeros((128, 4), np.float32)
        g4[:, 0] = inp["mla_g_qn"][l]
        g4[:64, 1] = inp["mla_g_qr"][l]
        g4[:, 2] = inp["mla_g_kn"][l]
        g4[:64, 3] = inp["mla_g_kr"][l]
        m["mla_g4"] = g4
        maps.append({k_: np.ascontiguousarray(v, dtype=np.float32) for k_, v in m.items()})
    return maps


def emit_B(nc, k, l, din, dint, dout, SH, mod, GTl):
    banks = SH["banks"]
    GD, GALL = SH["GD"], SH["GALL"]
    GTall = dint("GTall", [2 * 3072, NTB], BF16)
    Gm = dint("Gmine", [3072, NTB], BF16)
    hm = dint("hmine", [1024, NTB], BF16)
    hT2 = dint("hT2", [2 * 1024, NTB], BF16)
    GM, HM = SH["GM"], SH["HM"]
    Gv = [Gm[r_ * 1536:(r_ + 1) * 1536, :].rearrange("(c p) t -> p c t", p=128) for r_ in range(2)]
    hv = hm.rearrange("(c p) t -> p c t", p=128)
    x1s = dint("x1s", [1024, NTB], F32)
    if l == 0:
        xsrc = din("xTs", [1024, NTB]).rearrange("(c p) t -> p c t", p=128)
        xdst = x1s.rearrange("(c p) t -> p c t", p=128)
    else:
        xsrc = x1s.rearrange("(c p) t -> p c t", p=128)
        xdst = dout("xo", [1024, NTB], F32).rearrange("(c p) t -> p c t", p=128)
    wgm_d = din(f"wgm_{l}", [1024, 3072]).rearrange("(c p) n -> p c n", p=128)
    wd_d = [din(f"{nm}_{l}", [1024, 1024]).rearrange("(c p) n -> p c n", p=128) for nm in ("wda", "wdb", "wdc", "wout")]
    HD = SH["HDall"]
    XS, XO = SH["XS"], SH["XO"]
    with ExitStack() as sb:
        ps = Ring(banks).get
        wgm = k.sbuf([128, 8, 3072], BF16, "wgm", sb)
        for kc in range(8):
            k.dma(wgm[:, kc, :], wgm_d[:, kc, :], W=[wgm], q='gpsimd')
        wd = [k.sbuf([128, 8, 1024], BF16, f"wd{i}", sb) for i in range(4)]
        for i in range(4):
            for kc in range(0, 8, 4):
                k.dma(wd[i][:, kc:kc + 4, :], wd_d[i][:, kc:kc + 4, :], W=[wd[i]], q='gpsimd')
        def exchange():
            CH = 256
            GTall = dint("GTall", [2 * 3072, NTB], BF16)
            half = nc.sync.partition_id() % 2
            Gm = dint("Gmine", [3072, NTB], BF16)
            hm = dint("hmine", [1024, NTB], BF16)
            hT2 = dint("hT2", [2 * 1024, NTB], BF16)
            GM, HM = SH["GM"], SH["HM"]
            GTall4 = GTall.rearrange("(c r q) t -> c r q t", r=2, q=CH)
            for r_ in range(2):
                src = GTall4[bass.ds(half * 6, 6), r_, :, :]
                k.dma(Gm[r_ * 1536:(r_ + 1) * 1536, :].rearrange("(c q) t -> c q t", q=CH), src, R=[GALL], W=[GM])
            k.dma(hm[:, :], hT2[bass.ds(half * 1024, 1024), :], R=SH["HDall"], W=[HM])
        exchange()
        Gr = Ring([k.sbuf([128, 24, 256], BF16, f"G{i}", sb) for i in range(2)])
        hr = Ring([k.sbuf([128, 8, 256], BF16, f"h{i}", sb) for i in range(2)])
        xr = Ring([k.sbuf([128, 8, 256], F32, f"x{i}", sb) for i in range(1)])
        xor_ = Ring([k.sbuf([128, 8, 256], F32, f"xo{i}", sb) for i in range(1)])
        m = k.sbuf([128, 24, 256], BF16, "m", sb)
        z = k.sbuf([128, 8, 256], BF16, "z", sb)
        t1r = Ring([k.sbuf([128, 256], F32, f"t1_{i}", sb) for i in range(2)])
        t2r = Ring([k.sbuf([128, 256], F32, f"t2_{i}", sb) for i in range(2)])
        for ti, (s0_, N) in enumerate(TB):
            col = 1 if ti == 0 else 0
            G = Gr.get(); h = hr.get(); x = xr.get(); xo = xor_.get()
            for r_ in range(2):
                k.dma(G[:, r_ * 12:(r_ + 1) * 12, :N], Gv[r_][:, :, s0_:s0_ + N], R=[GM], W=[G])
            k.dma(h[:, :, :N], hv[:, :, s0_:s0_ + N], R=[HM], W=[h])
            k.dma(x[:, :, :N], xsrc[:, :, s0_:s0_ + N], R=[XS], W=[x])
            for j in range(24):
                p = ps()
                for kc in range(8):
                    k.op('tensor', lambda e, kc=kc: e.matmul(p[:, :N], lhsT=wgm[:, kc, j * 128:(j + 1) * 128], rhs=h[:, kc, :N], start=(kc == 0), stop=(kc == 7)), R=[wgm, h], W=[p], inc=(kc == 7))
                k.op('scalar', lambda e: e.activation(out=m[:, j, :N], in_=p[:, :N], func=AF.Sigmoid), R=[p], W=[m])
            for oc in range(8):
                pp = [ps(), ps(), ps()]
                for br in range(3):
                    for kc in range(8):
                        gi = (kc // 4) * 12 + br * 4 + (kc % 4)
                        k.op('tensor', lambda e, br=br, kc=kc, gi=gi: e.matmul(pp[br][:, :N], lhsT=wd[br][:, kc, oc * 128:(oc + 1) * 128], rhs=G[:, gi, :N], start=(kc == 0), stop=(kc == 7)),
                             R=[wd[br], G], W=[pp[br]])
                t1 = t1r.get(); t2 = t2r.get()
                k.op('vector', lambda e: e.tensor_tensor(out=t1[:, :N], in0=pp[0][:, :N], in1=m[:, oc, :N], op=ALU.mult), R=[pp[0], m], W=[t1])
                k.op('vector', lambda e: e.tensor_tensor(out=t2[:, :N], in0=pp[1][:, :N], in1=m[:, 8 + oc, :N], op=ALU.mult), R=[pp[1], m], W=[t2])
                k.op('vector', lambda e: e.tensor_tensor(out=t1[:, :N], in0=t1[:, :N], in1=t2[:, :N], op=ALU.add), R=[t1, t2], W=[t1])
                k.op('vector', lambda e: e.tensor_tensor(out=t2[:, :N], in0=pp[2][:, :N], in1=m[:, 16 + oc, :N], op=ALU.mult), R=[pp[2], m], W=[t2])
                k.op('vector', lambda e: e.tensor_tensor(out=z[:, oc, :N], in0=t1[:, :N], in1=t2[:, :N], op=ALU.add), R=[t1, t2], W=[z])
            for oc in range(8):
                p = ps()
                for kc in range(8):
                    k.op('tensor', lambda e, kc=kc: e.matmul(p[:, :N], lhsT=wd[3][:, kc, oc * 128:(oc + 1) * 128], rhs=z[:, kc, :N], start=(kc == 0), stop=(kc == 7)), R=[wd[3], z], W=[p])
                k.op('vector', lambda e: e.scalar_tensor_tensor(out=xo[:, oc, :N], in0=p[:, :N], scalar=mod[:, 16 + oc, col:col + 1], op0=ALU.mult, in1=x[:, oc, :N], op1=ALU.add), R=[p, mod, x], W=[xo])
            k.dma(xdst[:, :, s0_:s0_ + N], xo[:, :, :N], R=[xo], W=[XO if l == 1 else XS])
        k.barrier()
    if l == 0:
        x1all = dint("x1all", [2 * 1024, NTB], F32)
        for c_ in range(8):
            k.collective("AllGather", PAIRS, x1s[c_ * 128:(c_ + 1) * 128, :], x1all[c_ * 256:(c_ + 1) * 256, :], R=[XS], W=[SH["X1ALL"]])


PAIRS = [[0, 1], [2, 3], [4, 5], [6, 7]]
_DECLARED = set()


def build_F(upto=None):
    nc = bass.Bass("TRN2", target_bir_lowering=False)
    dram = {}

    def din(name, shape, dt=F32):
        if name not in dram:
            dram[name] = nc.dram_tensor(name, list(shape), dt, kind="ExternalInput").ap()
            _DECLARED.add(name)
        return dram[name]

    def dint(name, shape, dt):
        if name not in dram:
            dram[name] = nc.dram_tensor(name, list(shape), dt, kind="Internal").ap()
        return dram[name]

    def dout(name, shape, dt):
        if name not in dram:
            dram[name] = nc.dram_tensor(name, list(shape), dt, kind="ExternalOutput").ap()
        return dram[name]

    with ExitStack() as st:
        k = K(nc, st)
        SH = {}
        SH["banks"] = [k.psum([128, 512], F32, f"ps{i}") for i in range(8)]
        ones = k.sbuf([128, 128], BF16, "ones")
        k.op('vector', lambda e: e.memset(ones[:], 1.0), W=[ones])
        SH["ones"] = ones
        for nm in ("ident", "permR", "permM"):
            t = k.sbuf([128, 128], BF16, nm)
            k.dma(t[:], din("c_" + nm, [128, 128]), W=[t], q='gpsimd')
            SH[nm] = t
        SH["hring"] = Ring([k.sbuf([128, 8, 512], BF16, f"hr{i}") for i in range(2)])
        for nm in ("GD", "GALL", "X1ALL", "XS", "XO", "GM", "HM"):
            SH[nm] = k.fake(nm)
        stop = False
        for l in range(2):
            SH["HDall"] = []
            mod, GTl = emit_A(nc, k, l, din, dint, SH)
            if upto == f"A{l}":
                stop = True
                break
            if upto == f"A{l}c":
                GTall = dint("GTall", [2 * 3072, NTB], BF16)
                k.barrier()
                stop = True
                break
            emit_B(nc, k, l, din, dint, dout, SH, mod, GTl)
            if upto == f"B{l}":
                stop = True
                break
        if stop and "xo" not in dram:
            xo_ = dout("xo", [1024, NTB], F32)
            k.barrier()
            k.dma(xo_[0:128, 0:48], mod[:].rearrange("p a b -> p (a b)"), R=[mod], W=[SH["XO"]])
        k.finish([SH["XO"]])
        print("F: ops", k.n_ops, "sems", k.nsem)
    return nc


def _tok_idx(p):
    return np.concatenate([np.arange(p * 128, (p + 1) * 128), 256 + np.arange(p * 2048, (p + 1) * 2048)])


def kernel(**inp):
    inp = {k_: np.asarray(v) for k_, v in inp.items()}
    consts = _consts()
    xT_all = [np.ascontiguousarray(np.concatenate([inp["ctx"][b], inp["x"][b]], 0).T) for b in range(4)]
    per_layer = [a_inputs(inp, l, xT_all, consts) for l in range(2)]
    maps = []
    for core in range(8):
        b, p = core // 2, core % 2
        m = {}
        for l in range(2):
            for k_, v in per_layer[l][core].items():
                if k_.startswith("c_") or k_ == "xT":
                    m[k_] = v
                else:
                    m[f"{k_}_{l}"] = v
            m[f"wgm_{l}"] = np.ascontiguousarray(inp["w_in"][l][:, 7872:10944])
            m[f"wda_{l}"] = inp["w_down_a"][l]
            m[f"wdb_{l}"] = inp["w_down_b"][l]
            m[f"wdc_{l}"] = inp["w_down_c"][l]
            m[f"wout_{l}"] = inp["w_out"][l]
        m["xTs"] = np.ascontiguousarray(xT_all[b][:, _tok_idx(p)])
        maps.append(m)
    nc = build_F()
    used = set(t for t in _DECLARED)
    maps = [{k_: v for k_, v in m.items() if k_ in used} for m in maps]
    res = run_bass_kernel_spmd(nc, maps, core_ids=list(range(8))).results
    out = np.empty((4, 4096, 1024), np.float32)
    for core in range(8):
        b, p = core // 2, core % 2
        out[b, p * 2048:(p + 1) * 2048, :] = res[core]["xo"][:, 128:].T
    return out
```
